# Optimizing a Trainium2 kernel written in Bass

```python
import math
import jax
import jax.numpy as jnp
from jax import lax
import numpy as np

D_MODEL = 2048
BATCH = 8
SEQ = 4096
DEPTH = 4

N_MIXERS = 2
N_RWKV = (DEPTH + 1) // 2
N_GDN = DEPTH // 2
ALPHA = (2 * DEPTH) ** 0.25
BETA = (8 * DEPTH) ** -0.25
LN_EPS = 1e-5
NORM_EPS = 1e-6

RW_HEAD = 64
RW_HEADS = D_MODEL // RW_HEAD
RW_LORA_DECAY = max(32, int(round(1.8 * D_MODEL ** 0.5 / 32)) * 32)
RW_LORA_AAA = max(32, int(round(1.8 * D_MODEL ** 0.5 / 32)) * 32)
RW_LORA_MV = max(32, int(round(1.3 * D_MODEL ** 0.5 / 32)) * 32)
RW_LORA_GATE = max(32, int(round(0.6 * D_MODEL ** 0.8 / 32)) * 32)
RW_LNX_EPS = 64e-5
MU_R = 0
MU_W = 1
MU_K = 2
MU_V = 3
MU_A = 4
MU_G = 5

GDN_HEAD_K = 128
GDN_HEAD_V = 128
GDN_K_HEADS = D_MODEL // GDN_HEAD_K
GDN_V_HEADS = 2 * GDN_K_HEADS
GDN_KEY_DIM = GDN_K_HEADS * GDN_HEAD_K
GDN_VALUE_DIM = GDN_V_HEADS * GDN_HEAD_V
GDN_CONV_CH = 2 * GDN_KEY_DIM + GDN_VALUE_DIM
GDN_PROJ = GDN_CONV_CH + GDN_VALUE_DIM + 2 * GDN_V_HEADS
GDN_CONV = 4
GDN_CHUNK = 64

D_FF = ((8 * D_MODEL // 3 + 127) // 128) * 128
FFN_CONV = 3

kernel_name = 'hybrid_rwkv7_gdn_convffn_deepnorm'


def layer_norm(x, g, b):
    xf = x.astype(jnp.float32)
    mu = jnp.mean(xf, -1, keepdims=True)
    var = jnp.mean(jnp.square(xf - mu), -1, keepdims=True)
    return ((xf - mu) * lax.rsqrt(var + LN_EPS) * g + b).astype(x.dtype)


def l2_normalize(x):
    xf = x.astype(jnp.float32)
    return (xf * lax.rsqrt(jnp.sum(xf * xf, -1, keepdims=True) + NORM_EPS)).astype(x.dtype)


def causal_dwconv(x, w):
    K = w.shape[0]
    T = x.shape[1]
    xp = jnp.pad(x, ((0, 0), (K - 1, 0), (0, 0)))
    out = w[K - 1] * x
    for k in range(K - 1):
        out = out + w[k] * xp[:, k:k + T]
    return out


def rwkv7_recurrence(r, w, k, v, a, b):
    B, T, H, N = r.shape
    decay = jnp.exp(-jnp.exp(w.astype(jnp.float32)))
    to_time = lambda t: jnp.moveaxis(t.astype(jnp.float32), 1, 0)

    def step(S, inp):
        r_t, d_t, k_t, v_t, a_t, b_t = inp
        sa = jnp.einsum('bhij,bhj->bhi', S, a_t)
        S = S * d_t[:, :, None, :] + sa[..., None] * b_t[:, :, None, :] + v_t[..., None] * k_t[:, :, None, :]
        return S, jnp.einsum('bhij,bhj->bhi', S, r_t)

    S0 = jnp.zeros((B, H, N, N), jnp.float32)
    _, y = lax.scan(step, S0, (to_time(r), jnp.moveaxis(decay, 1, 0), to_time(k), to_time(v), to_time(a), to_time(b)))
    return jnp.moveaxis(y, 0, 1).astype(r.dtype)


def rwkv7_time_mix(x, mu, w_r, w_k, w_v, w_o, w0, w1, w2, a0, a1, a2, g1, g2,
                   k_k, k_a, r_k, lnx_g, lnx_b, v_first, vres):
    B, T, D = x.shape
    H, N = RW_HEADS, RW_HEAD
    heads = lambda t: t.reshape(B, T, H, N)
    x_prev = jnp.pad(x, ((0, 0), (1, 0), (0, 0)))[:, :T]
    xx = x_prev - x
    xr = x + xx * mu[MU_R]
    xw = x + xx * mu[MU_W]
    xk = x + xx * mu[MU_K]
    xv = x + xx * mu[MU_V]
    xa = x + xx * mu[MU_A]
    xg = x + xx * mu[MU_G]

    r = xr @ w_r
    w = -jax.nn.softplus(-(w0 + jnp.tanh(xw @ w1) @ w2)) - 0.5
    k = xk @ w_k
    v = xv @ w_v
    if vres is None:
        v_first = v
    else:
        v0, v1, v2 = vres
        v = v + (v_first - v) * jax.nn.sigmoid(v0 + (xv @ v1) @ v2)
    a = jax.nn.sigmoid(a0 + (xa @ a1) @ a2)
    g = jax.nn.sigmoid(xg @ g1) @ g2

    kk = l2_normalize(heads(k * k_k))
    k = k * (1.0 + (a - 1.0) * k_a)
    y = rwkv7_recurrence(heads(r), heads(w), heads(k), heads(v), -kk, kk * heads(a))

    yf = y.astype(jnp.float32)
    mean = jnp.mean(yf, -1, keepdims=True)
    var = jnp.mean(jnp.square(yf - mean), -1, keepdims=True)
    y = (yf - mean) * lax.rsqrt(var + RW_LNX_EPS) * lnx_g.reshape(H, N) + lnx_b.reshape(H, N)
    y = y + jnp.sum(heads(r) * heads(k) * r_k, -1, keepdims=True) * heads(v)
    return (y.reshape(B, T, D).astype(x.dtype) * g) @ w_o, v_first


def chunk_gated_delta_rule(q, k, v, g, beta):
    B, T, H, DK = q.shape
    DV = v.shape[-1]
    C = GDN_CHUNK
    NC = T // C

    def chunks(t):
        t = jnp.moveaxis(t.astype(jnp.float32), 2, 1)
        return t.reshape((B, H, NC, C) + t.shape[3:])

    q, k, v, g, beta = chunks(q), chunks(k), chunks(v), chunks(g), chunks(beta)
    g = jnp.cumsum(g, axis=-1)
    idx = jnp.arange(C)
    causal = idx[:, None] >= idx[None, :]
    strict = idx[:, None] > idx[None, :]

    k_beta = k * beta[..., None]
    v_beta = v * beta[..., None]
    diff = g[..., :, None] - g[..., None, :]
    decay = jnp.exp(jnp.where(causal, diff, 0.0))
    A = jnp.where(strict, jnp.einsum('bhnik,bhnjk->bhnij', k_beta, k) * decay, 0.0)
    eye = jnp.broadcast_to(jnp.eye(C, dtype=jnp.float32), A.shape)
    Tm = lax.linalg.triangular_solve(A + eye, eye, left_side=True, lower=True, unit_diagonal=True)
    u = Tm @ v_beta
    wk = Tm @ (k_beta * jnp.exp(g)[..., None])

    def step(S, inp):
        q_n, k_n, u_n, w_n, g_n = inp
        d_n = g_n[..., :, None] - g_n[..., None, :]
        attn = jnp.where(causal, jnp.einsum('bhik,bhjk->bhij', q_n, k_n) * jnp.exp(jnp.where(causal, d_n, 0.0)), 0.0)
        v_new = u_n - jnp.einsum('bhck,bhkv->bhcv', w_n, S)
        o = (jnp.einsum('bhck,bhkv->bhcv', q_n * jnp.exp(g_n)[..., None], S)
             + jnp.einsum('bhij,bhjv->bhiv', attn, v_new))
        g_last = g_n[..., -1]
        S = (S * jnp.exp(g_last)[..., None, None]
             + jnp.einsum('bhck,bhcv->bhkv', k_n * jnp.exp(g_last[..., None] - g_n)[..., None], v_new))
        return S, o

    xs = tuple(jnp.moveaxis(t, 2, 0) for t in (q, k, u, wk, g))
    S0 = jnp.zeros((B, H, DK, DV), jnp.float32)
    _, o = lax.scan(step, S0, xs)
    o = jnp.moveaxis(o, 0, 2).reshape(B, H, T, DV)
    return jnp.moveaxis(o, 1, 2)


def gated_deltanet(x, w_in, conv_w, a_log, dt_bias, norm_w, w_out):
    B, T, D = x.shape
    proj = x @ w_in
    qkv, z, b, a = jnp.split(proj, [GDN_CONV_CH, GDN_CONV_CH + GDN_VALUE_DIM,
                                    GDN_CONV_CH + GDN_VALUE_DIM + GDN_V_HEADS], axis=-1)
    qkv = jax.nn.silu(causal_dwconv(qkv, conv_w))
    q, k, v = jnp.split(qkv, [GDN_KEY_DIM, 2 * GDN_KEY_DIM], axis=-1)
    rep = GDN_V_HEADS // GDN_K_HEADS
    q = jnp.repeat(l2_normalize(q.reshape(B, T, GDN_K_HEADS, GDN_HEAD_K)), rep, axis=2) * (GDN_HEAD_K ** -0.5)
    k = jnp.repeat(l2_normalize(k.reshape(B, T, GDN_K_HEADS, GDN_HEAD_K)), rep, axis=2)
    v = v.reshape(B, T, GDN_V_HEADS, GDN_HEAD_V)
    beta = jax.nn.sigmoid(b.astype(jnp.float32))
    g = -jnp.exp(a_log.astype(jnp.float32)) * jax.nn.softplus((a + dt_bias).astype(jnp.float32))
    o = chunk_gated_delta_rule(q, k, v, g, beta)
    o = o * lax.rsqrt(jnp.mean(o * o, -1, keepdims=True) + NORM_EPS) * norm_w
    o = o * jax.nn.silu(z.reshape(B, T, GDN_V_HEADS, GDN_HEAD_V).astype(jnp.float32))
    return o.reshape(B, T, GDN_VALUE_DIM).astype(x.dtype) @ w_out


def conv_ffn(x, w_up, conv_w, conv_b, w_down):
    h = causal_dwconv(x @ w_up, conv_w) + conv_b
    gate, up = jnp.split(h, 2, axis=-1)
    return (jax.nn.silu(gate) * up) @ w_down


def setup_inputs(seed: int = 0) -> dict:
    key = jax.random.key(seed)
    keys = iter(jax.random.split(key, 40))

    def nrm(shape, scale):
        return scale * jax.random.normal(next(keys), shape, jnp.float32)

    def uni(shape, lo, hi):
        return jax.random.uniform(next(keys), shape, jnp.float32, lo, hi)

    D, R, G, L = D_MODEL, N_RWKV, N_GDN, DEPTH
    x = nrm((BATCH, SEQ, D), 1.0)
    rw_mu = uni((R, 6, D), 0.0, 1.0)
    rw_w_r = nrm((R, D, D), D ** -0.5)
    rw_w_k = nrm((R, D, D), D ** -0.5)
    rw_w_v = nrm((R, D, D), D ** -0.5)
    rw_w_o = nrm((R, D, D), BETA * D ** -0.5)
    rw_w0 = uni((R, D), -6.0, 0.0)
    rw_w1 = nrm((R, D, RW_LORA_DECAY), D ** -0.5)
    rw_w2 = nrm((R, RW_LORA_DECAY, D), 0.5 * RW_LORA_DECAY ** -0.5)
    rw_a0 = nrm((R, D), 0.5)
    rw_a1 = nrm((R, D, RW_LORA_AAA), D ** -0.5)
    rw_a2 = nrm((R, RW_LORA_AAA, D), RW_LORA_AAA ** -0.5)
    rw_v0 = nrm((R - 1, D), 0.5)
    rw_v1 = nrm((R - 1, D, RW_LORA_MV), D ** -0.5)
    rw_v2 = nrm((R - 1, RW_LORA_MV, D), RW_LORA_MV ** -0.5)
    rw_g1 = nrm((R, D, RW_LORA_GATE), D ** -0.5)
    rw_g2 = nrm((R, RW_LORA_GATE, D), RW_LORA_GATE ** -0.5)
    rw_k_k = 0.85 + nrm((R, D), 0.05)
    rw_k_a = 1.0 + nrm((R, D), 0.05)
    rw_r_k = nrm((R, RW_HEADS, RW_HEAD), 0.1)
    rw_lnx_g = 1.0 + nrm((R, D), 0.05)
    rw_lnx_b = nrm((R, D), 0.02)
    gdn_w_in = nrm((G, D, GDN_PROJ), D ** -0.5)
    gdn_conv_w = nrm((G, GDN_CONV, GDN_CONV_CH), GDN_CONV ** -0.5)
    gdn_a_log = jnp.log(uni((G, GDN_V_HEADS), 1.0, 16.0))
    dt = jnp.exp(uni((G, GDN_V_HEADS), math.log(1e-3), math.log(1e-1)))
    gdn_dt_bias = dt + jnp.log(-jnp.expm1(-dt))
    gdn_norm_w = 1.0 + nrm((G, GDN_HEAD_V), 0.05)
    gdn_w_out = nrm((G, GDN_VALUE_DIM, D), BETA * GDN_VALUE_DIM ** -0.5)
    ffn_w_up = nrm((L, D, 2 * D_FF), D ** -0.5)
    ffn_conv_w = nrm((L, FFN_CONV, 2 * D_FF), FFN_CONV ** -0.5)
    ffn_conv_b = nrm((L, 2 * D_FF), 0.02)
    ffn_w_down = nrm((L, D_FF, D), BETA * D_FF ** -0.5)
    ln_mix_g = 1.0 + nrm((L, D), 0.05)
    ln_mix_b = nrm((L, D), 0.02)
    ln_ffn_g = 1.0 + nrm((L, D), 0.05)
    ln_ffn_b = nrm((L, D), 0.02)
    return {'x': x, 'rw_mu': rw_mu, 'rw_w_r': rw_w_r, 'rw_w_k': rw_w_k, 'rw_w_v': rw_w_v,
            'rw_w_o': rw_w_o, 'rw_w0': rw_w0, 'rw_w1': rw_w1, 'rw_w2': rw_w2, 'rw_a0': rw_a0,
            'rw_a1': rw_a1, 'rw_a2': rw_a2, 'rw_v0': rw_v0, 'rw_v1': rw_v1, 'rw_v2': rw_v2,
            'rw_g1': rw_g1, 'rw_g2': rw_g2, 'rw_k_k': rw_k_k, 'rw_k_a': rw_k_a, 'rw_r_k': rw_r_k,
            'rw_lnx_g': rw_lnx_g, 'rw_lnx_b': rw_lnx_b, 'gdn_w_in': gdn_w_in,
            'gdn_conv_w': gdn_conv_w, 'gdn_a_log': gdn_a_log, 'gdn_dt_bias': gdn_dt_bias,
            'gdn_norm_w': gdn_norm_w, 'gdn_w_out': gdn_w_out, 'ffn_w_up': ffn_w_up,
            'ffn_conv_w': ffn_conv_w, 'ffn_conv_b': ffn_conv_b, 'ffn_w_down': ffn_w_down,
            'ln_mix_g': ln_mix_g, 'ln_mix_b': ln_mix_b, 'ln_ffn_g': ln_ffn_g, 'ln_ffn_b': ln_ffn_b}


def reference(x, rw_mu, rw_w_r, rw_w_k, rw_w_v, rw_w_o, rw_w0, rw_w1, rw_w2, rw_a0,
              rw_a1, rw_a2, rw_v0, rw_v1, rw_v2, rw_g1, rw_g2, rw_k_k, rw_k_a, rw_r_k,
              rw_lnx_g, rw_lnx_b, gdn_w_in, gdn_conv_w, gdn_a_log, gdn_dt_bias,
              gdn_norm_w, gdn_w_out, ffn_w_up, ffn_conv_w, ffn_conv_b, ffn_w_down,
              ln_mix_g, ln_mix_b, ln_ffn_g, ln_ffn_b):
    v_first = None
    for i in range(DEPTH):
        j = i // N_MIXERS
        if i % N_MIXERS == 0:
            vres = None if j == 0 else (rw_v0[j - 1], rw_v1[j - 1], rw_v2[j - 1])
            mix, v_first = rwkv7_time_mix(
                x, rw_mu[j], rw_w_r[j], rw_w_k[j], rw_w_v[j], rw_w_o[j], rw_w0[j], rw_w1[j],
                rw_w2[j], rw_a0[j], rw_a1[j], rw_a2[j], rw_g1[j], rw_g2[j], rw_k_k[j],
                rw_k_a[j], rw_r_k[j], rw_lnx_g[j], rw_lnx_b[j], v_first, vres)
        else:
            mix = gated_deltanet(x, gdn_w_in[j], gdn_conv_w[j], gdn_a_log[j],
                                 gdn_dt_bias[j], gdn_norm_w[j], gdn_w_out[j])
        x = layer_norm(ALPHA * x + mix, ln_mix_g[i], ln_mix_b[i])
        x = layer_norm(ALPHA * x + conv_ffn(x, ffn_w_up[i], ffn_conv_w[i], ffn_conv_b[i], ffn_w_down[i]),
                       ln_ffn_g[i], ln_ffn_b[i])
    return x
```

```python
import numpy as np
import concourse.bass as bass
import concourse.mybir as mybir
from concourse.bass_utils import run_bass_kernel_spmd

F32 = mybir.dt.float32
BF16 = mybir.dt.bfloat16
ALU = mybir.AluOpType
AF = mybir.ActivationFunctionType

D = 2048
DC = 16
SEQ = 4096
DEPTH = 4
ALPHA = (2 * DEPTH) ** 0.25
LN_EPS = 1e-5
NORM_EPS = 1e-6
DFF = 5504
FC = 43
TT = 512

ENGS = ["tensor", "vector", "scalar", "gpsimd", "sync"]
SAME_ENGINE_SYNC = True
DEBUG_TAGS = False


class Buf:
    __slots__ = ("name", "lw", "rd")

    def __init__(self, name):
        self.name = name
        self.lw = None
        self.rd = []


class Op:
    __slots__ = ("eng", "fn", "deps", "idx", "sig", "sigcnt", "is_dma", "dma_sem", "dma_cnt", "tag")


class Tile:
    def __init__(self, prog, name, shape, dtype, nb=1, psum=False):
        nc = prog.nc
        if psum:
            self.h = nc.alloc_psum_tensor(name, list(shape), dtype)
        else:
            self.h = nc.alloc_sbuf_tensor(name, list(shape), dtype)
        self.bufs = [Buf(f"{name}.{i}") for i in range(nb)]
        self.b = self.bufs[0]
        self.shape = shape

    def __getitem__(self, k):
        return self.h[k]


class Prog:
    def __init__(self, nc):
        self.nc = nc
        self.streams = {e: [] for e in ENGS}
        self.esem = {}
        self.dsem = {}
        self.dcnt = {}
        self.n_ops = 0
        self._grp = None

    def group_begin(self):
        assert self._grp is None
        self._grp = []

    def group_end(self):
        members = set(o for o, _ in self._grp)
        for o, key in self._grp:
            o.dma_cnt = self.dcnt[key]
            o.deps = o.deps - members
        self._grp = None

    def tile(self, name, shape, dtype, nb=1):
        return Tile(self, name, shape, dtype, nb)

    def psum(self, name, shape, dtype=F32):
        return Tile(self, name, shape, dtype, 1, psum=True)

    def _track(self, o, reads, writes):
        deps = set()
        for b in reads:
            if b.lw is not None:
                deps.add(b.lw)
        for b in writes:
            if b.lw is not None:
                deps.add(b.lw)
            for r in b.rd:
                deps.add(r)
        deps.discard(o)
        o.deps = deps
        for b in reads:
            b.rd.append(o)
        for b in writes:
            b.lw = o
            b.rd = []

    def op(self, eng, fn, reads=(), writes=()):
        o = Op()
        o.eng = eng
        o.fn = fn
        o.sig = False
        o.sigcnt = 0
        o.is_dma = False
        o.idx = len(self.streams[eng])
        if DEBUG_TAGS:
            import sys as _s
            f = _s._getframe(1)
            tg = []
            for _ in range(4):
                if f is None:
                    break
                tg.append(str(f.f_lineno))
                f = f.f_back
            o.tag = "/".join(tg)
        self._track(o, reads, writes)
        self.streams[eng].append(o)
        self.n_ops += 1
        return o

    def dma(self, out, in_, reads=(), writes=(), key=None, eng="sync", slow=False):
        o = Op()
        o.eng = eng
        if slow:
            o.fn = lambda e, out=out, in_=in_: e.dma_start(out=out, in_=in_, allow_slow_non_contiguous=True)
        else:
            o.fn = lambda e, out=out, in_=in_: e.dma_start(out=out, in_=in_)
        o.sig = False
        o.sigcnt = 0
        o.is_dma = True
        o.idx = len(self.streams[eng])
        if key not in self.dsem:
            self.dsem[key] = self.nc.alloc_semaphore(f"d{len(self.dsem)}")
            self.dcnt[key] = 0
        self.dcnt[key] += 16
        o.dma_sem = self.dsem[key]
        o.dma_cnt = self.dcnt[key]
        if self._grp is not None:
            self._grp.append((o, key))
        self._track(o, reads, writes)
        self.streams[eng].append(o)
        self.n_ops += 1
        return o

    def emit(self):
        nc = self.nc
        for e in ENGS:
            self.esem[e] = nc.alloc_semaphore(f"e_{e}")
        for e in ENGS:
            for o in self.streams[e]:
                for d in o.deps:
                    if not d.is_dma:
                        if d.eng == o.eng and (d.eng == "tensor" or not SAME_ENGINE_SYNC):
                            continue
                        d.sig = True
        for e in ENGS:
            c = 0
            for o in self.streams[e]:
                if o.sig:
                    c += 1
                o.sigcnt = c
        prog = self
        with nc.Block() as block:
            def make(engname):
                def body(e):
                    waited = {}
                    for o in prog.streams[engname]:
                        need = {}
                        for d in o.deps:
                            if d.is_dma:
                                k, v = d.dma_sem, d.dma_cnt
                            else:
                                if d.eng == engname and (engname == "tensor" or not SAME_ENGINE_SYNC):
                                    continue
                                k, v = prog.esem[d.eng], d.sigcnt
                            if need.get(k, 0) < v:
                                need[k] = v
                        for k, v in need.items():
                            if waited.get(k, 0) < v:
                                e.wait_ge(k, v)
                                waited[k] = v
                        ins = o.fn(e)
                        if DEBUG_TAGS and not o.is_dma:
                            ins.annotate(o.tag)
                        if o.is_dma:
                            ins.then_inc(o.dma_sem, 16)
                        elif o.sig:
                            ins.then_inc(prog.esem[engname], 1)
                    if engname == "sync":
                        for k, sem in prog.dsem.items():
                            if waited.get(sem, 0) < prog.dcnt[k]:
                                e.wait_ge(sem, prog.dcnt[k])
                return body
            block.tensor(make("tensor"))
            block.vector(make("vector"))
            block.scalar(make("scalar"))
            block.gpsimd(make("gpsimd"))
            block.sync(make("sync"))


class WSpec:
    def __init__(self, nc, name, src, K, N):
        self.name = name
        self.src = src
        self.K = K
        self.N = N
        self.KC = (K + 127) // 128
        self.NCH = (N + 127) // 128
        self.dst = nc.dram_tensor(f"wb_{name}", [self.NCH, 128, self.KC * 128], BF16, kind="Internal").ap()
        self.buf = Buf(f"wb_{name}")


class V:
    def __init__(self, ap, bufs):
        self.ap = ap
        self.bufs = list(bufs)
        self.b = self.bufs[0]

    def __getitem__(self, k):
        return self.ap[k]


NS = 82
RW_EXPM = float(np.exp(-0.5))


class Builder:
    def __init__(self, T, plan):
        self.T = T
        self.NT = T // TT
        self.plan = plan
        nc = bass.Bass("TRN2", target_bir_lowering=False)
        self.nc = nc
        self.P = Prog(nc)
        self.inputs = {}
        self.wspecs = {}
        self.cv_i = 0
        self._dbufs = {}
        self.rr = 0
        self.ps_i = 0
        self.ws_i = 0

    def din(self, name, shape):
        ap = self.nc.dram_tensor(name, list(shape), F32, kind="ExternalInput").ap()
        self.inputs[name] = ap
        return ap

    def dscratch(self, name, shape, dtype=F32):
        return self.nc.dram_tensor(name, list(shape), dtype, kind="Internal").ap()

    def wspec(self, name, src, K, N):
        w = WSpec(self.nc, name, src, K, N)
        self.wspecs[name] = w
        return w

    def dbuf(self, *key):
        if key not in self._dbufs:
            self._dbufs[key] = Buf("dram")
        return self._dbufs[key]

    def setup_common(self):
        P = self.P
        self.AR = P.tile("arena", [128, NS, 512], F32, nb=NS)
        self.ps = [P.psum(f"ps{i}", [128, 512]) for i in range(8)]
        self.wslots = [P.tile(f"wsl{i}", [128, 2048], BF16) for i in range(5)]
        self.ones = P.tile("ones", [128, 128], F32)
        self.ident = P.tile("ident", [128, 128], F32)
        self.bones = P.tile("bones", [128, 128], F32)
        self.mask2 = P.tile("mask2", [128, 256], F32)
        self.maskT = P.tile("maskT", [128, 256], F32)
        self.xprev = P.tile("xprev", [128, DC], F32)
        g = "gpsimd"
        P.op(g, lambda e: e.memset(self.ones[:], 1.0), writes=[self.ones.b])
        P.op(g, lambda e: e.memset(self.bones[:], 0.0), writes=[self.bones.b])
        P.op(g, lambda e: e.memset(self.bones[0:64, 0:64], 1.0), writes=[self.bones.b])
        P.op(g, lambda e: e.memset(self.bones[64:128, 64:128], 1.0), writes=[self.bones.b])
        P.op(g, lambda e: e.affine_select(out=self.ident[:], in_=self.ones[:], pattern=[[-1, 128]], compare_op=ALU.is_equal,
                                          fill=0.0, base=0, channel_multiplier=1), reads=[self.ones.b], writes=[self.ident.b])
        P.op(g, lambda e: e.affine_select(out=self.mask2[:, 0:128], in_=self.ones[:], pattern=[[1, 128]], compare_op=ALU.is_gt,
                                          fill=0.0, base=0, channel_multiplier=-1), reads=[self.ones.b], writes=[self.mask2.b])
        P.op(g, lambda e: e.affine_select(out=self.mask2[:, 128:256], in_=self.ones[:], pattern=[[1, 128]], compare_op=ALU.is_ge,
                                          fill=0.0, base=0, channel_multiplier=-1), reads=[self.ones.b], writes=[self.mask2.b])
        P.op(g, lambda e: e.affine_select(out=self.maskT[:, 0:128], in_=self.ones[:], pattern=[[-1, 128]], compare_op=ALU.is_gt,
                                          fill=0.0, base=0, channel_multiplier=1), reads=[self.ones.b], writes=[self.maskT.b])
        P.op(g, lambda e: e.affine_select(out=self.maskT[:, 128:256], in_=self.ones[:], pattern=[[-1, 128]], compare_op=ALU.is_ge,
                                          fill=0.0, base=0, channel_multiplier=1), reads=[self.ones.b], writes=[self.maskT.b])

    def f32(self, s0, n):
        return V(self.AR[:, s0:s0 + n, :], self.AR.bufs[s0:s0 + n])

    def sl(self, s):
        return V(self.AR[:, s, :], [self.AR.bufs[s]])

    def flat(self, s0, n):
        return V(self.AR[:, s0:s0 + n, :].rearrange("p s t -> p (s t)"), self.AR.bufs[s0:s0 + n])

    def bf(self, s0, nch):
        ns = (nch + 1) // 2
        ap = self.AR[:, s0:s0 + ns, :].bitcast(BF16).rearrange("p s (h t) -> p (s h) t", h=2)
        return V(ap, [self.AR.bufs[s0 + c // 2] for c in range(2 * ns)])

    def next_ps(self):
        t = self.ps[self.ps_i % 8]
        self.ps_i += 1
        return t

    def ew(self):
        self.rr += 1
        return ["vector", "gpsimd"][self.rr % 2]

    def mm(self, ps, out, lhsT, rhs, start, stop, reads):
        self.P.op("tensor", lambda e: e.matmul(out, lhsT=lhsT, rhs=rhs, start=start, stop=stop), reads=reads, writes=[ps.b])

    def tr(self, ps, out, in_, reads):
        K = in_.shape[0]
        idn = self.ident
        self.P.op("tensor", lambda e: e.transpose(out, in_, idn[0:K, 0:K]), reads=list(reads) + [idn.b], writes=[ps.b])

    def act(self, out, in_, func, reads, writes, bias=None, scale=None):
        kw = {}
        if bias is not None:
            kw["bias"] = bias
        if scale is not None:
            kw["scale"] = scale
        self.P.op("scalar", lambda e: e.activation(out=out, in_=in_, func=func, **kw), reads=reads, writes=writes)

    def tt(self, eng, out, in0, in1, op, reads, writes):
        self.P.op(eng, lambda e: e.tensor_tensor(out=out, in0=in0, in1=in1, op=op), reads=reads, writes=writes)

    def ts(self, eng, out, in0, s1, s2, op0, op1, reads, writes):
        if s2 is None:
            self.P.op(eng, lambda e: e.tensor_scalar(out=out, in0=in0, scalar1=s1, scalar2=None, op0=op0), reads=reads, writes=writes)
        else:
            self.P.op(eng, lambda e: e.tensor_scalar(out=out, in0=in0, scalar1=s1, scalar2=s2, op0=op0, op1=op1), reads=reads, writes=writes)

    def stt(self, eng, out, in0, scalar, in1, op0, op1, reads, writes):
        eng = "vector"
        self.P.op(eng, lambda e: e.scalar_tensor_tensor(out=out, in0=in0, scalar=scalar, in1=in1, op0=op0, op1=op1), reads=reads, writes=writes)

    def cp(self, eng, out, in_, reads, writes):
        if eng == "scalar":
            self.P.op(eng, lambda e: e.copy(out=out, in_=in_), reads=reads, writes=writes)
        else:
            self.P.op(eng, lambda e: e.tensor_copy(out=out, in_=in_), reads=reads, writes=writes)

    def rsqrt(self, out, in_, eps, reads, writes, mul=1.0):
        self.act(out, in_, AF.Ln, reads, writes, bias=float(eps), scale=float(mul))
        self.act(out, out, AF.Exp, writes, writes, scale=-0.5)

    def convert(self, w):
        P = self.P
        R = w.KC * 128
        cvf = [self.flat(66, 2), self.flat(68, 2)]
        cvb = [self.bf(70, 2), self.bf(71, 2)]
        for n in range(w.NCH):
            for c0 in range(0, R, 1024):
                cw = min(1024, R - c0)
                i = self.cv_i % 2
                self.cv_i += 1
                tf, tb = cvf[i], cvb[i]
                tbf = tb.ap.rearrange("p a t -> p (a t)")
                P.dma(tf[:, 0:cw], w.src[n, :, c0:c0 + cw], writes=tf.bufs, key=("cv", i))
                ce = ["gpsimd", "vector", "scalar"][self.cv_i % 3]
                self.cp(ce, tbf[:, 0:cw], tf[:, 0:cw], tf.bufs, [tb.b])
                P.dma(w.dst[n, :, c0:c0 + cw], tbf[:, 0:cw], reads=[tb.b], writes=[w.buf], key=("cvst", i))

    def wload(self, w, n, kc0=0, nkc=None):
        P = self.P
        if nkc is None:
            nkc = w.KC
        t = self.wslots[self.ws_i % len(self.wslots)]
        self.ws_i += 1
        P.dma(t[:, 0:nkc * 128], w.dst[n, :, kc0 * 128:(kc0 + nkc) * 128], reads=[w.buf], writes=[t.b], key=t.b)
        return t

    def linear(self, w, n, rhs_fn, nk=None, M=128):
        ps = self.next_ps()
        KC = w.KC if nk is None else nk
        for k0 in range(0, KC, 16):
            nkc = min(16, KC - k0)
            wt = self.wload(w, n, k0, nkc)
            for kk in range(nkc):
                kc = k0 + kk
                rows = min(128, w.K - kc * 128)
                rap, rb = rhs_fn(kc)
                self.mm(ps, ps[0:M, :], wt[0:rows, kk * 128:kk * 128 + M], rap, kc == 0, kc == KC - 1, [wt.b] + list(rb))
        return ps

    def layernorm(self, S, g, b):
        P = self.P
        ps_s = self.next_ps()
        ps_q = self.next_ps()
        ones = self.ones
        sqs = [self.sl(58), self.sl(59)]
        mean, rstd, nmr = self.sl(60), self.sl(61), self.sl(62)
        tmps = [self.sl(63), self.sl(64)]
        for c in range(DC):
            sq = sqs[c % 2]
            self.act(sq[:], S[:, c, :], AF.Square, [S.bufs[c]], [sq.b])
            self.mm(ps_s, ps_s[:], ones[:], S[:, c, :], c == 0, c == DC - 1, [S.bufs[c], ones.b])
            self.mm(ps_q, ps_q[:], ones[:], sq[:], c == 0, c == DC - 1, [sq.b, ones.b])
        self.ts("vector", mean[:], ps_s[:], 1.0 / D, None, ALU.mult, None, [ps_s.b], [mean.b])
        self.tt("vector", nmr[:], mean[:], mean[:], ALU.mult, [mean.b], [nmr.b])
        self.stt("vector", rstd[:], ps_q[:], 1.0 / D, nmr[:], ALU.mult, ALU.subtract, [ps_q.b, nmr.b], [rstd.b])
        self.rsqrt(rstd[:], rstd[:], LN_EPS, [rstd.b], [rstd.b])
        self.stt("vector", nmr[:], mean[:], -1.0, rstd[:], ALU.mult, ALU.mult, [mean.b, rstd.b], [nmr.b])
        for c in range(DC):
            tmp = tmps[c % 2]
            eng = self.ew()
            self.tt(eng, tmp[:], S[:, c, :], rstd[:], ALU.mult, [S.bufs[c], rstd.b], [tmp.b])
            self.tt(eng, tmp[:], tmp[:], nmr[:], ALU.add, [tmp.b, nmr.b], [tmp.b])
            self.act(S[:, c, :], tmp[:], AF.Identity, [tmp.b, g.b, b.b], [S.bufs[c]], bias=b[:, c:c + 1], scale=g[:, c:c + 1])

    def load_x(self, X, src, sname, ti):
        t0 = ti * TT
        self.P.group_begin()
        for c in range(DC):
            self.P.dma(X[:, c, :], src[c, :, t0:t0 + TT], reads=[self.dbuf(sname, c, ti)], writes=[X.bufs[c]], key="ldx")
        self.P.group_end()

    def store_x(self, X, dst, dname, ti):
        t0 = ti * TT
        self.P.group_begin()
        for c in range(DC):
            self.P.dma(dst[c, :, t0:t0 + TT], X[:, c, :], reads=[X.bufs[c]], writes=[self.dbuf(dname, c, ti)], key="stx")
        self.P.group_end()

    def ld(self, dstv, src_ap, dkey, key=None):
        self.P.dma(dstv.ap if isinstance(dstv, V) else dstv, src_ap, reads=[self.dbuf(*dkey)], writes=dstv.bufs,
                   key=("ld", dstv.b) if key is None else key)

    def st(self, dst_ap, srcv, dkey, src_ap=None):
        self.P.dma(dst_ap, srcv.ap if src_ap is None else src_ap, reads=srcv.bufs, writes=[self.dbuf(*dkey)], key=("st", srcv.b))

    def ffn_stage(self, L, xin, xin_name, xout, xout_name):
        P = self.P
        cv = self.cv
        X = self.f32(0, 16)
        Xb = self.bf(16, 16)
        f_act = self.bf(24, FC)
        hb = [[self.flat(46, 2), self.flat(48, 2)], [self.flat(50, 2), self.flat(52, 2)]]
        ob = [[self.sl(54), self.sl(55)], [self.sl(56), self.sl(57)]]
        w_up, w_dn = self.wspecs[f"up{L}"], self.wspecs[f"dn{L}"]
        cw, cb = cv[f"fcw{L}"], cv[f"fcb{L}"]
        NH = 2 * FC
        if not hasattr(self, "f_carry"):
            self.f_carry = P.tile("f_carry", [128, 2 * FC, 2], F32, nb=2 * FC)
        carry = self.f_carry
        for ti in range(self.NT):
            self.load_x(X, xin, xin_name, ti)
            for c in range(DC):
                self.cp(self.ew(), Xb[:, c, :], X[:, c, :], [X.bufs[c]], [Xb.bufs[c]])
            for j in range(FC):
                outs = []
                for half in (0, 1):
                    hbuf = hb[half][j % 2]
                    obuf = ob[half][j % 2]
                    n = j + half * FC
                    ps = self.linear(w_up, n, lambda kc: (Xb[:, kc, :], [Xb.bufs[kc]]))
                    car = carry.bufs[n]
                    if ti == 0:
                        P.op("gpsimd", lambda e, hbuf=hbuf: e.memset(hbuf[:, 0:2], 0.0), writes=hbuf.bufs)
                    else:
                        self.cp("gpsimd", hbuf[:, 0:2], carry[:, n, :], [car], hbuf.bufs)
                    self.cp("scalar", hbuf[:, 2:514], ps[:], [ps.b], hbuf.bufs)
                    self.cp("gpsimd", carry[:, n, :], hbuf[:, 512:514], hbuf.bufs, [car])
                    self.ts("vector", obuf[:], hbuf[:, 2:514], cw[:, 2 * NH + n], cb[:, n], ALU.mult, ALU.add,
                            hbuf.bufs + [cw.b], [obuf.b])
                    self.stt("vector", obuf[:], hbuf[:, 1:513], cw[:, NH + n], obuf[:], ALU.mult, ALU.add,
                             hbuf.bufs + [cw.b, obuf.b], [obuf.b])
                    self.stt("gpsimd", obuf[:], hbuf[:, 0:512], cw[:, n], obuf[:], ALU.mult, ALU.add,
                             hbuf.bufs + [cw.b, obuf.b], [obuf.b])
                    outs.append(obuf)
                og, ou = outs
                self.act(og[:], og[:], AF.Silu, [og.b], [og.b])
                self.tt("vector", f_act[:, j, :], og[:], ou[:], ALU.mult, [og.b, ou.b], [f_act.bufs[j]])
            for n in range(DC):
                ps = self.linear(w_dn, n, lambda kc: (f_act[:, kc, :], [f_act.bufs[kc]]))
                self.stt("vector", X[:, n, :], X[:, n, :], ALPHA, ps[:], ALU.mult, ALU.add, [ps.b, X.bufs[n]], [X.bufs[n]])
            self.layernorm(X, cv[f"flg{L}"], cv[f"flb{L}"])
            self.store_x(X, xout, xout_name, ti)

    def rwkv_setup_dram(self):
        if hasattr(self, "rw_d"):
            return
        T = self.T
        self.rw_d = {nm: self.dscratch("rw_" + nm, [DC, 128, T]) for nm in ("R", "KP", "V", "LD", "KK", "BV", "G", "VF")}
        self.rw_yg = self.dscratch("rw_YG", [DC, 128, T], BF16)

    def rwkv_stage(self, L, xin, xin_name, xout, xout_name):
        self.rwkv_setup_dram()
        self.rwkv_phase1(L, xin, xin_name)
        self.rwkv_phase2(L)
        self.rwkv_phase3(L, xin, xin_name, xout, xout_name)

    def rwkv_phase1(self, L, xin, xin_name):
        P = self.P
        j = L // 2
        cv = self.cv
        W = self.wspecs
        mu = cv[f"rmu{j}"]
        X = self.f32(0, 16)
        MIX = {"r": self.bf(16, 16), "k": self.bf(24, 16), "v": self.bf(32, 16), "t": self.bf(40, 16)}
        LOR = self.bf(48, 6)
        xx = [self.sl(51), self.sl(52)]
        outs = {nm: [self.sl(53 + 2 * i), self.sl(54 + 2 * i)] for i, nm in enumerate(("R", "KP", "V", "LD", "KK", "BV", "G"))}
        tmp = [self.sl(72 + i) for i in range(10)]
        xprev = self.xprev
        rd = self.rw_d
        omka = cv[f"romka{j}"]
        if not hasattr(self, "omka_done"):
            self.omka_done = set()
        if j not in self.omka_done:
            self.omka_done.add(j)
            ka = cv[f"rka{j}"]
            self.ts("vector", omka[:, :], ka[:, :], -1.0, 1.0, ALU.mult, ALU.add, [ka.b], [omka.b])

        def mix(m, mi, c):
            dst = MIX[m]
            self.stt(self.ew(), dst[:, c, :], xx[c % 2][:], mu[:, mi * DC + c], X[:, c, :], ALU.mult, ALU.add,
                     [xx[c % 2].b, X.bufs[c], mu.b], [dst.bufs[c]])

        for ti in range(self.NT):
            t0 = ti * TT
            self.load_x(X, xin, xin_name, ti)
            for c in range(DC):
                x_ = xx[c % 2]
                self.tt("vector", x_[:, 1:512], X[:, c, 0:511], X[:, c, 1:512], ALU.subtract, [X.bufs[c]], [x_.b])
                if ti == 0:
                    self.ts("vector", x_[:, 0:1], X[:, c, 0:1], -1.0, None, ALU.mult, None, [X.bufs[c]], [x_.b])
                else:
                    self.tt("vector", x_[:, 0:1], xprev[:, c:c + 1], X[:, c, 0:1], ALU.subtract, [X.bufs[c], xprev.b], [x_.b])
                mix("r", 0, c)
                mix("k", 2, c)
                mix("v", 3, c)
            for (mi, wname, lo, func, M) in ((1, f"rw1{j}", 0, AF.Tanh, 96), (4, f"ra1{j}", 1, AF.Copy, 96),
                                               (5, f"rg1{j}", 2, AF.Sigmoid, 128)) + (((3, f"rv1{j}", 4, AF.Copy, 64),) if j > 0 else ()):
                if mi != 3:
                    for c in range(DC):
                        x_ = xx[c % 2]
                        self.tt("vector", x_[:, 1:512], X[:, c, 0:511], X[:, c, 1:512], ALU.subtract, [X.bufs[c]], [x_.b])
                        if ti == 0:
                            self.ts("vector", x_[:, 0:1], X[:, c, 0:1], -1.0, None, ALU.mult, None, [X.bufs[c]], [x_.b])
                        else:
                            self.tt("vector", x_[:, 0:1], xprev[:, c:c + 1], X[:, c, 0:1], ALU.subtract, [X.bufs[c], xprev.b], [x_.b])
                        mix("t", mi, c)
                    src = MIX["t"]
                else:
                    src = MIX["v"]
                w = W[wname]
                for n in range(w.NCH):
                    ps = self.linear(w, n, lambda kc, src=src: (src[:, kc, :], [src.bufs[kc]]), M=M)
                    self.act(LOR[0:M, lo + n, :], ps[0:M, :], func, [ps.b], [LOR.bufs[lo + n]])
            for c in range(DC):
                self.cp("gpsimd", xprev[:, c:c + 1], X[:, c, 511:512], [X.bufs[c]], [xprev.b])
            for n in range(DC):
                q = n % 2
                o = {nm: outs[nm][q] for nm in outs}
                ps_r = self.linear(W[f"rwr{j}"], n, lambda kc: (MIX["r"][:, kc, :], [MIX["r"].bufs[kc]]))
                self.cp("scalar", o["R"][:], ps_r[:], [ps_r.b], [o["R"].b])
                self.st(rd["R"][n, :, t0:t0 + TT], o["R"], ("rwR", n, ti))
                ps_k = self.linear(W[f"rwk{j}"], n, lambda kc: (MIX["k"][:, kc, :], [MIX["k"].bufs[kc]]))
                kraw = tmp[0]
                self.cp("scalar", kraw[:], ps_k[:], [ps_k.b], [kraw.b])
                ps_v = self.linear(W[f"rwv{j}"], n, lambda kc: (MIX["v"][:, kc, :], [MIX["v"].bufs[kc]]))
                ps_w = self.linear(W[f"rw2{j}"], n, lambda kc: (LOR[0:96, 0, :], [LOR.bufs[0]]))
                self.act(o["LD"][:], ps_w[:], AF.Sigmoid, [ps_w.b, cv[f"rw0{j}"].b], [o["LD"].b], bias=cv[f"rw0{j}"][:, n])
                self.ts("gpsimd", o["LD"][:], o["LD"][:], -RW_EXPM, None, ALU.mult, None, [o["LD"].b], [o["LD"].b])
                self.st(rd["LD"][n, :, t0:t0 + TT], o["LD"], ("rwLD", n, ti))
                ps_a = self.linear(W[f"ra2{j}"], n, lambda kc: (LOR[0:96, 1, :], [LOR.bufs[1]]))
                A = tmp[1]
                self.act(A[:], ps_a[:], AF.Sigmoid, [ps_a.b, cv[f"ra0{j}"].b], [A.b], bias=cv[f"ra0{j}"][:, n])
                ps_g = self.linear(W[f"rg2{j}"], n, lambda kc: (LOR[:, 2 + kc, :], [LOR.bufs[2 + kc]]))
                self.cp("scalar", o["G"][:], ps_g[:], [ps_g.b], [o["G"].b])
                self.st(rd["G"][n, :, t0:t0 + TT], o["G"], ("rwG", n, ti))
                if j == 0:
                    self.cp("scalar", o["V"][:], ps_v[:], [ps_v.b], [o["V"].b])
                    self.st(rd["VF"][n, :, t0:t0 + TT], o["V"], ("rwVF", n, ti))
                else:
                    ps_l = self.linear(W[f"rv2{j}"], n, lambda kc: (LOR[0:64, 4, :], [LOR.bufs[4]]))
                    sg, vf, vr = tmp[2], tmp[3], tmp[4]
                    self.act(sg[:], ps_l[:], AF.Sigmoid, [ps_l.b, cv[f"rv0{j}"].b], [sg.b], bias=cv[f"rv0{j}"][:, n])
                    self.ld(vf, rd["VF"][n, :, t0:t0 + TT], ("rwVF", n, ti))
                    self.cp("scalar", vr[:], ps_v[:], [ps_v.b], [vr.b])
                    self.tt("vector", vf[:], vf[:], vr[:], ALU.subtract, [vf.b, vr.b], [vf.b])
                    self.tt("gpsimd", vf[:], vf[:], sg[:], ALU.mult, [vf.b, sg.b], [vf.b])
                    self.tt("vector", o["V"][:], vf[:], vr[:], ALU.add, [vf.b, vr.b], [o["V"].b])
                self.st(rd["V"][n, :, t0:t0 + TT], o["V"], ("rwV", n, ti))
                kku, sq, rn = tmp[5], tmp[6], tmp[7]
                self.ts("vector", kku[:], kraw[:], cv[f"rkk{j}"][:, n], None, ALU.mult, None, [kraw.b, cv[f"rkk{j}"].b], [kku.b])
                self.tt("gpsimd", sq[:], kku[:], kku[:], ALU.mult, [kku.b], [sq.b])
                ps_n = self.next_ps()
                self.mm(ps_n, ps_n[:], self.bones[:], sq[:], True, True, [self.bones.b, sq.b])
                self.cp("vector", rn[:], ps_n[:], [ps_n.b], [rn.b])
                self.rsqrt(rn[:], rn[:], NORM_EPS, [rn.b], [rn.b])
                self.tt("vector", o["KK"][:], kku[:], rn[:], ALU.mult, [kku.b, rn.b], [o["KK"].b])
                self.st(rd["KK"][n, :, t0:t0 + TT], o["KK"], ("rwKK", n, ti))
                self.tt("gpsimd", o["BV"][:], o["KK"][:], A[:], ALU.mult, [o["KK"].b, A.b], [o["BV"].b])
                self.st(rd["BV"][n, :, t0:t0 + TT], o["BV"], ("rwBV", n, ti))
                t1 = tmp[8]
                self.ts("vector", t1[:], A[:], cv[f"rka{j}"][:, n], omka[:, n], ALU.mult, ALU.add, [A.b, cv[f"rka{j}"].b, omka.b], [t1.b])
                self.tt("gpsimd", o["KP"][:], kraw[:], t1[:], ALU.mult, [kraw.b, t1.b], [o["KP"].b])
                self.st(rd["KP"][n, :, t0:t0 + TT], o["KP"], ("rwKP", n, ti))

    def rwkv_phase2(self, L):
        P = self.P
        j = L // 2
        cv = self.cv
        rd = self.rw_d
        names = ("R", "KP", "V", "LD", "KK", "BV", "G")
        IN = [{nm: self.sl(2 * i + q) for i, nm in enumerate(names)} for q in range(2)]
        CS, EP, EN, EPM, ED = (self.sl(14 + i) for i in range(5))
        ARt = self.flat(19, 2)
        BTt, KTt, BH, KH, TMP = (self.sl(21 + i) for i in range(5))
        VTOK, BHTOK, KHTOK, YTOK = (self.sl(26 + i) for i in range(4))
        ATS = [[self.flat(30, 2), self.flat(32, 2)], [self.flat(74, 2), self.flat(76, 2)]]
        PW = [self.sl(34), self.sl(35)]
        Q = self.sl(36)
        WU = self.sl(37)
        HBD = self.sl(38)
        PT = [self.sl(39 + i) for i in range(6)]
        YGo = [self.bf(45, 2), self.bf(78, 2)]
        ident = self.ident
        mask2, maskT = self.mask2, self.maskT
        m2b = mask2[:].unsqueeze(1).to_broadcast([128, 2, 256]) if False else None
        for c in range(DC):
            P.op("gpsimd", lambda e: e.memset(HBD[:, 0:128], 0.0), writes=[HBD.b])
            for ti in range(self.NT):
                t0 = ti * TT
                I = IN[ti % 2]
                P.group_begin()
                for nm in names:
                    self.ld(I[nm], rd[nm][c, :, t0:t0 + TT], ("rw" + nm, c, ti), key=("ldin", ti % 2))
                P.group_end()
                R, KP, Vv, LDv, KK, BV, G = (I[nm] for nm in names)
                for ch in range(4):
                    cs_ = slice(ch * 128, (ch + 1) * 128)
                    P.op("vector", lambda e, o_=CS[:, cs_], d1=LDv[:, cs_]: e.tensor_tensor_scan(
                        out=o_, data0=self.ones[:, 0:128], data1=d1, initial=0.0, op0=ALU.mult, op1=ALU.add),
                         reads=[LDv.b, self.ones.b], writes=[CS.b])
                self.act(EP[:], CS[:], AF.Exp, [CS.b], [EP.b])
                self.act(EN[:], CS[:], AF.Exp, [CS.b], [EN.b], scale=-1.0)
                self.tt("gpsimd", TMP[:], CS[:], LDv[:], ALU.subtract, [CS.b, LDv.b], [TMP.b])
                self.act(EPM[:], TMP[:], AF.Exp, [TMP.b], [EPM.b])
                for ch in range(4):
                    cs_ = slice(ch * 128, (ch + 1) * 128)
                    self.act(ED[:, cs_], CS[:, cs_], AF.Exp, [CS.b], [ED.b], scale=-1.0, bias=CS[:, ch * 128 + 127:ch * 128 + 128])
                    self.stt("vector", ARt[:, ch * 256:ch * 256 + 128], KK[:, cs_], -1.0, EPM[:, cs_], ALU.mult, ALU.mult,
                             [KK.b, EPM.b], ARt.bufs)
                    self.tt("gpsimd", ARt[:, ch * 256 + 128:ch * 256 + 256], R[:, cs_], EP[:, cs_], ALU.mult, [R.b, EP.b], ARt.bufs)
                self.tt("vector", BTt[:], BV[:], EN[:], ALU.mult, [BV.b, EN.b], [BTt.b])
                self.tt("gpsimd", KTt[:], KP[:], EN[:], ALU.mult, [KP.b, EN.b], [KTt.b])
                self.tt("vector", BH[:], BV[:], ED[:], ALU.mult, [BV.b, ED.b], [BH.b])
                self.tt("gpsimd", KH[:], KP[:], ED[:], ALU.mult, [KP.b, ED.b], [KH.b])
                for srcv, dstv in ((Vv, VTOK), (BH, BHTOK), (KH, KHTOK)):
                    ps = self.next_ps()
                    for ch in range(4):
                        cs_ = slice(ch * 128, (ch + 1) * 128)
                        self.tr(ps, ps[:, cs_], srcv[:, cs_], [srcv.b])
                    self.cp("scalar", dstv[:], ps[:], [ps.b], [dstv.b])
                for ch in range(4):
                    cs_ = slice(ch * 128, (ch + 1) * 128)
                    p = ch % 2
                    ATb, ATk = ATS[p]
                    ps_b = self.next_ps()
                    ps_k = self.next_ps()
                    ps_a = self.next_ps()
                    for h in range(2):
                        hp = slice(64 * h, 64 * h + 64)
                        self.mm(ps_b, ps_b[:, h * 256:h * 256 + 256], BTt[hp, cs_], ARt[hp, ch * 256:ch * 256 + 256], True, True,
                                [BTt.b] + ARt.bufs)
                        self.mm(ps_k, ps_k[:, h * 256:h * 256 + 256], KTt[hp, cs_], ARt[hp, ch * 256:ch * 256 + 256], True, True,
                                [KTt.b] + ARt.bufs)
                        self.mm(ps_a, ps_a[:, h * 256:h * 256 + 128], ARt[hp, ch * 256:ch * 256 + 128], BTt[hp, cs_], True, True,
                                [BTt.b] + ARt.bufs)
                    for h in range(2):
                        self.tt("vector", ATb[:, h * 256:h * 256 + 256], ps_b[:, h * 256:h * 256 + 256], mask2[:], ALU.mult,
                                [ps_b.b, mask2.b], ATb.bufs)
                        self.tt("vector", ATk[:, h * 256:h * 256 + 256], ps_k[:, h * 256:h * 256 + 256], mask2[:], ALU.mult,
                                [ps_k.b, mask2.b], ATk.bufs)
                    pw = PW[0]
                    for h in range(2):
                        self.tt("vector", pw[:, h * 256:h * 256 + 128], ps_a[:, h * 256:h * 256 + 128], maskT[:, 0:128], ALU.mult,
                                [ps_a.b, maskT.b], [pw.b])
                        self.cp("gpsimd", pw[:, h * 256 + 128:h * 256 + 256], ATb[:, h * 256:h * 256 + 128], ATb.bufs, [pw.b])
                        self.tt("gpsimd", Q[:, h * 128:h * 128 + 128], ATb[:, h * 256:h * 256 + 128], ident[:], ALU.add,
                                ATb.bufs + [ident.b], [Q.b])
                    self.neumann(PW, Q)
                    ps_w = self.next_ps()
                    self.mm(ps_w, ps_w[:, 0:128], ARt[:, ch * 256:ch * 256 + 128], HBD[:, 0:128], True, False, ARt.bufs + [HBD.b])
                    for h in range(2):
                        self.mm(ps_w, ps_w[:, h * 64:h * 64 + 64], ATk[:, h * 256:h * 256 + 128], VTOK[:, ch * 128 + h * 64:ch * 128 + h * 64 + 64],
                                False, h == 1, ATk.bufs + [VTOK.b])
                    self.cp("vector", WU[:, 0:128], ps_w[:, 0:128], [ps_w.b], [WU.b])
                    ps_u = self.next_ps()
                    for h in range(2):
                        self.mm(ps_u, ps_u[:, h * 64:h * 64 + 64], Q[:, h * 128:h * 128 + 128], WU[:, h * 64:h * 64 + 64], True, True,
                                [Q.b, WU.b])
                    self.cp("vector", WU[:, 128:256], ps_u[:, 0:128], [ps_u.b], [WU.b])
                    ps_y = self.next_ps()
                    self.mm(ps_y, ps_y[:, 0:128], ARt[:, ch * 256 + 128:ch * 256 + 256], HBD[:, 0:128], True, False, ARt.bufs + [HBD.b])
                    for h in range(2):
                        self.mm(ps_y, ps_y[:, h * 64:h * 64 + 64], ATk[:, h * 256 + 128:h * 256 + 256],
                                VTOK[:, ch * 128 + h * 64:ch * 128 + h * 64 + 64], False, False, ATk.bufs + [VTOK.b])
                    for h in range(2):
                        self.mm(ps_y, ps_y[:, h * 64:h * 64 + 64], ATb[:, h * 256 + 128:h * 256 + 256], WU[:, 128 + h * 64:128 + h * 64 + 64],
                                False, h == 1, ATb.bufs + [WU.b])
                    self.cp("scalar", YTOK[:, cs_], ps_y[:, 0:128], [ps_y.b], [YTOK.b])
                    ps_h = self.next_ps()
                    self.mm(ps_h, ps_h[:, 0:128], BHTOK[:, cs_], WU[:, 128:256], True, False, [BHTOK.b, WU.b])
                    self.mm(ps_h, ps_h[:, 0:128], KHTOK[:, cs_], VTOK[:, cs_], False, True, [KHTOK.b, VTOK.b])
                    for h in range(2):
                        hp = slice(64 * h, 64 * h + 64)
                        self.stt("vector", HBD[hp, 64 * h:64 * h + 64], HBD[hp, 64 * h:64 * h + 64], EP[hp, ch * 128 + 127:ch * 128 + 128],
                                 ps_h[hp, 64 * h:64 * h + 64], ALU.mult, ALU.add, [HBD.b, EP.b, ps_h.b], [HBD.b])
                mean8, var8 = PT[0], PT[1]
                CEN, SQ = PT[2], PT[3]
                Y3 = YTOK[:].rearrange("p (g v) -> p g v", v=64)
                C3 = CEN[:].rearrange("p (g v) -> p g v", v=64)
                S3 = SQ[:].rearrange("p (g v) -> p g v", v=64)
                P.op("vector", lambda e, o_=mean8[:, 0:8], i_=Y3: e.reduce_sum(out=o_, in_=i_, axis=mybir.AxisListType.X), reads=[YTOK.b], writes=[mean8.b])
                self.ts("vector", mean8[:, 0:8], mean8[:, 0:8], 1.0 / 64, None, ALU.mult, None, [mean8.b], [mean8.b])
                self.tt("vector", C3, Y3, mean8[:, 0:8].unsqueeze(2).to_broadcast([128, 8, 64]), ALU.subtract, [YTOK.b, mean8.b], [CEN.b])
                self.tt("gpsimd", SQ[:], CEN[:], CEN[:], ALU.mult, [CEN.b], [SQ.b])
                P.op("vector", lambda e, o_=var8[:, 0:8], i_=S3: e.reduce_sum(out=o_, in_=i_, axis=mybir.AxisListType.X), reads=[SQ.b], writes=[var8.b])
                self.rsqrt(var8[:, 0:8], var8[:, 0:8], 64e-5, [var8.b], [var8.b], mul=1.0 / 64)
                self.tt("vector", C3, C3, var8[:, 0:8].unsqueeze(2).to_broadcast([128, 8, 64]), ALU.mult, [CEN.b, var8.b], [CEN.b])
                ps_t = self.next_ps()
                for ch in range(4):
                    cs_ = slice(ch * 128, (ch + 1) * 128)
                    self.tr(ps_t, ps_t[:, cs_], CEN[:, cs_], [CEN.b])
                YA = PT[4]
                self.act(YA[:], ps_t[:], AF.Identity, [ps_t.b, cv[f"rlg{j}"].b, cv[f"rlb{j}"].b], [YA.b],
                         bias=cv[f"rlb{j}"][:, c], scale=cv[f"rlg{j}"][:, c])
                RK = PT[5]
                self.stt("vector", RK[:], R[:], cv[f"rrk{j}"][:, c], KP[:], ALU.mult, ALU.mult, [R.b, KP.b, cv[f"rrk{j}"].b], [RK.b])
                ps_bn = self.next_ps()
                self.mm(ps_bn, ps_bn[:], self.bones[:], RK[:], True, True, [self.bones.b, RK.b])
                self.tt("vector", RK[:], ps_bn[:], Vv[:], ALU.mult, [ps_bn.b, Vv.b], [RK.b])
                self.tt("gpsimd", YA[:], YA[:], RK[:], ALU.add, [YA.b, RK.b], [YA.b])
                yg = YGo[ti % 2]
                self.tt("vector", yg[:, 0, :], YA[:], G[:], ALU.mult, [YA.b, G.b], [yg.b])
                self.P.dma(self.rw_yg[c, :, t0:t0 + TT], yg[:, 0, :], reads=[yg.b], writes=[self.dbuf("rwYG", c, ti)], key=("st", yg.b))

    def rwkv_phase3(self, L, xin, xin_name, xout, xout_name):
        j = L // 2
        cv = self.cv
        X = self.f32(0, 16)
        YG = self.bf(16, 16)
        w_o = self.wspecs[f"rwo{j}"]
        for ti in range(self.NT):
            t0 = ti * TT
            self.load_x(X, xin, xin_name, ti)
            self.P.group_begin()
            for c in range(DC):
                self.P.dma(YG[:, c, :], self.rw_yg[c, :, t0:t0 + TT], reads=[self.dbuf("rwYG", c, ti)], writes=[YG.bufs[c]],
                           key="ldyg")
            self.P.group_end()
            for n in range(DC):
                ps = self.linear(w_o, n, lambda kc: (YG[:, kc, :], [YG.bufs[kc]]))
                self.stt("vector", X[:, n, :], X[:, n, :], ALPHA, ps[:], ALU.mult, ALU.add, [ps.b, X.bufs[n]], [X.bufs[n]])
            self.layernorm(X, cv[f"mlg{L}"], cv[f"mlb{L}"])
            self.store_x(X, xout, xout_name, ti)

    def neumann(self, PW, Q):
        for lvl in range(1, 7):
            src = PW[(lvl - 1) % 2]
            dst = PW[lvl % 2]
            psp = self.next_ps()
            for h in range(2):
                Ah = src[:, h * 256:h * 256 + 128]
                Nh = src[:, h * 256 + 128:h * 256 + 256]
                self.mm(psp, psp[:, h * 256:h * 256 + 128], Nh, Ah, True, True, [src.b])
                if lvl < 6:
                    self.mm(psp, psp[:, h * 256 + 128:h * 256 + 256], Ah, Nh, True, True, [src.b])
            if lvl < 6:
                self.cp("scalar", dst[:], psp[:], [psp.b], [dst.b])
            else:
                for h in range(2):
                    self.cp("scalar", dst[:, h * 256:h * 256 + 128], psp[:, h * 256:h * 256 + 128], [psp.b], [dst.b])
            psq = self.next_ps()
            for h in range(2):
                self.mm(psq, psq[:, h * 128:h * 128 + 128], dst[:, h * 256:h * 256 + 128], Q[:, h * 128:h * 128 + 128],
                        True, True, [dst.b, Q.b])
            self.tt("vector", Q[:, 0:256], Q[:, 0:256], psq[:, 0:256], ALU.add, [Q.b, psq.b], [Q.b])

    def gdn_setup_dram(self):
        if hasattr(self, "gd_d"):
            return
        T = self.T
        self.gd_d = {"QN": self.dscratch("gd_QN", [16, 128, T]), "KN": self.dscratch("gd_KN", [16, 128, T]),
                     "VV": self.dscratch("gd_VV", [32, 128, T]), "ZS": self.dscratch("gd_ZS", [32, 128, T]),
                     "BG": self.dscratch("gd_BG", [64, T])}
        self.gd_og = self.dscratch("gd_OG", [32, 128, T], BF16)
        P = self.P
        self.g_carry = P.tile("g_carry", [128, 64, 3], F32, nb=64)
        self.g_col = P.tile("g_col", [128, 2, 2, 2, 4], F32, nb=2)
        self.g_sc = P.tile("g_sc", [128, 2, 2, 4], F32, nb=2)
        self.negm = P.tile("negm", [128, 256], F32)
        self.ts("vector", self.negm[:, 0:128], self.mask2[:, 0:128], -1.0, None, ALU.mult, None, [self.mask2.b], [self.negm.b])
        self.ts("vector", self.negm[:, 128:256], self.maskT[:, 0:128], -1.0, None, ALU.mult, None, [self.maskT.b], [self.negm.b])

    def gdn_stage(self, L, xin, xin_name, xout, xout_name):
        self.gdn_setup_dram()
        self.gdn_phase1(L, xin, xin_name)
        self.gdn_phase2(L)
        self.gdn_phase3(L, xin, xin_name, xout, xout_name)

    def gdn_phase1(self, L, xin, xin_name):
        P = self.P
        j = L // 2
        cv = self.cv
        w_in = self.wspecs[f"gin{j}"]
        cw = cv[f"gcw{j}"]
        gd = self.gd_d
        X = self.f32(0, 16)
        Xb = self.bf(16, 16)
        hb = [self.flat(24, 2), self.flat(26, 2)]
        ob = [self.sl(28), self.sl(29)]
        t1 = [self.sl(30), self.sl(31)]
        t2 = [self.sl(32), self.sl(33)]
        bgt = self.sl(34)
        et = self.sl(35)
        carry = self.g_carry
        nea = cv[f"gnea{j}"]
        self.act(nea[:, :], cv[f"gal{j}"][:, :], AF.Exp, [cv[f"gal{j}"].b], [nea.b])
        self.ts("vector", nea[:, :], nea[:, :], -1.0, None, ALU.mult, None, [nea.b], [nea.b])
        for ti in range(self.NT):
            t0 = ti * TT
            self.load_x(X, xin, xin_name, ti)
            for c in range(DC):
                self.cp(self.ew(), Xb[:, c, :], X[:, c, :], [X.bufs[c]], [Xb.bufs[c]])
            rhs = lambda kc: (Xb[:, kc, :], [Xb.bufs[kc]])
            for n in range(64):
                hbuf, obuf = hb[n % 2], ob[n % 2]
                ps = self.linear(w_in, n, rhs)
                car = carry.bufs[n]
                if ti == 0:
                    P.op("gpsimd", lambda e, o_=hbuf[:, 0:3]: e.memset(o_, 0.0), writes=hbuf.bufs)
                else:
                    self.cp("gpsimd", hbuf[:, 0:3], carry[:, n, :], [car], hbuf.bufs)
                self.cp("scalar", hbuf[:, 3:515], ps[:], [ps.b], hbuf.bufs)
                self.cp("gpsimd", carry[:, n, :], hbuf[:, 512:515], hbuf.bufs, [car])
                self.ts("vector", obuf[:], hbuf[:, 3:515], cw[:, 3 * 64 + n], None, ALU.mult, None, hbuf.bufs + [cw.b], [obuf.b])
                for k, eng in ((2, "gpsimd"), (1, "vector"), (0, "gpsimd")):
                    self.stt(eng, obuf[:], hbuf[:, k:k + 512], cw[:, k * 64 + n], obuf[:], ALU.mult, ALU.add,
                             hbuf.bufs + [cw.b, obuf.b], [obuf.b])
                self.act(obuf[:], obuf[:], AF.Silu, [obuf.b], [obuf.b])
                if n < 32:
                    sq, rn = t1[n % 2], t2[n % 2]
                    self.tt("gpsimd", sq[:], obuf[:], obuf[:], ALU.mult, [obuf.b], [sq.b])
                    ps_n = self.next_ps()
                    self.mm(ps_n, ps_n[:], self.ones[:], sq[:], True, True, [self.ones.b, sq.b])
                    self.cp("vector", rn[:], ps_n[:], [ps_n.b], [rn.b])
                    self.rsqrt(rn[:], rn[:], NORM_EPS, [rn.b], [rn.b])
                    sc = (128.0 ** -0.5) if n < 16 else 1.0
                    self.stt("vector", obuf[:], obuf[:], sc, rn[:], ALU.mult, ALU.mult, [obuf.b, rn.b], [obuf.b])
                    nm, idx = ("QN", n) if n < 16 else ("KN", n - 16)
                else:
                    nm, idx = "VV", n - 32
                self.st(gd[nm][idx, :, t0:t0 + TT], obuf, ("gd" + nm, idx, ti))
            for n in range(64, 96):
                obuf = ob[n % 2]
                ps = self.linear(w_in, n, rhs)
                self.act(obuf[:], ps[:], AF.Silu, [ps.b], [obuf.b])
                self.st(gd["ZS"][n - 64, :, t0:t0 + TT], obuf, ("gdZS", n - 64, ti))
            ps = self.linear(w_in, 96, rhs, M=64)
            self.act(bgt[0:32, :], ps[0:32, :], AF.Sigmoid, [ps.b], [bgt.b])
            self.act(et[32:64, :], ps[32:64, :], AF.Exp, [ps.b, cv[f"gdt{j}"].b], [et.b], bias=cv[f"gdt{j}"][32:64, 0])
            self.act(et[32:64, :], et[32:64, :], AF.Ln, [et.b], [et.b], bias=1.0)
            self.ts("vector", et[32:64, :], et[32:64, :], nea[32:64, 0], None, ALU.mult, None, [et.b, nea.b], [et.b])
            for ch in range(4):
                cs_ = slice(ch * 128, (ch + 1) * 128)
                P.op("vector", lambda e, o_=bgt[32:64, cs_], d0=self.ones[32:64, 0:128], d1=et[32:64, cs_]: e.tensor_tensor_scan(
                    out=o_, data0=d0, data1=d1, initial=0.0, op0=ALU.mult, op1=ALU.add), reads=[et.b, self.ones.b], writes=[bgt.b])
            self.P.dma(gd["BG"][0:64, t0:t0 + TT], bgt[0:64, :], reads=[bgt.b], writes=[self.dbuf("gdBG", ti)], key=("st", bgt.b))

    def gdn_phase2(self, L):
        P = self.P
        j = L // 2
        cv = self.cv
        gd = self.gd_d
        INQ = [self.sl(0), self.sl(1)]
        INK = [self.sl(2), self.sl(3)]
        INV = [[self.sl(4), self.sl(5)], [self.sl(6), self.sl(7)]]
        INZ = [[self.sl(8), self.sl(9)], [self.sl(10), self.sl(11)]]
        ING = [[self.sl(12), self.sl(13)], [self.sl(14), self.sl(15)]]
        INB = [[self.sl(16), self.sl(17)], [self.sl(18), self.sl(19)]]
        EGB = [self.sl(20), self.sl(21)]
        QE = [self.sl(22), self.sl(23)]
        KTOK = self.sl(24)
        VTOK = [self.sl(25), self.sl(26)]
        SM = [self.flat(27, 2), self.flat(29, 2)]
        DD = [self.sl(31), self.sl(32)]
        AT = [self.sl(33), self.sl(34)]
        PW = [self.sl(37), self.sl(38)]
        Q = self.sl(39)
        UW = self.sl(40)
        VN = self.sl(41)
        S = self.sl(42)
        OTOK = [self.sl(44), self.sl(45)]
        PT = [self.sl(46 + i) for i in range(5)]
        OGo = [self.bf(51, 2), self.bf(52, 2)]
        ident = self.ident
        mask2, maskT, negm = self.mask2, self.maskT, self.negm
        gcol, gsc = self.g_col, self.g_sc
        bg = gd["BG"]
        nw = cv[f"gnw{j}"]
        for kh in range(16):
            P.op("gpsimd", lambda e, o_=S[:, 0:256]: e.memset(o_, 0.0), writes=[S.b])
            for ti in range(self.NT):
                t0 = ti * TT
                q = ti % 2
                Qt, Kt = INQ[q], INK[q]
                lk = ("ldin", q)
                P.group_begin()
                self.ld(Qt, gd["QN"][kh, :, t0:t0 + TT], ("gdQN", kh, ti), key=lk)
                self.ld(Kt, gd["KN"][kh, :, t0:t0 + TT], ("gdKN", kh, ti), key=lk)
                gcb = gcol.bufs[q]
                for hh in range(2):
                    h = 2 * kh + hh
                    self.ld(INV[q][hh], gd["VV"][h, :, t0:t0 + TT], ("gdVV", h, ti), key=lk)
                    self.ld(INZ[q][hh], gd["ZS"][h, :, t0:t0 + TT], ("gdZS", h, ti), key=lk)
                    self.ld(ING[q][hh], bg[32 + h:33 + h, t0:t0 + TT].partition_broadcast(128), ("gdBG", ti), key=lk)
                    self.ld(INB[q][hh], bg[h:h + 1, t0:t0 + TT].partition_broadcast(128), ("gdBG", ti), key=lk)
                    P.dma(gcol[:, q, hh, 0, :], bg[32 + h, t0:t0 + TT].rearrange("(c t) -> t c", t=128),
                          reads=[self.dbuf("gdBG", ti)], writes=[gcb], key=lk, slow=True)
                    P.dma(gcol[:, q, hh, 1, :], bg[h, t0:t0 + TT].rearrange("(c t) -> t c", t=128),
                          reads=[self.dbuf("gdBG", ti)], writes=[gcb], key=lk, slow=True)
                P.group_end()
                for hh in range(0):
                    pass
                for hh in range(2):
                    self.act(EGB[hh][:], ING[q][hh][:], AF.Exp, [ING[q][hh].b], [EGB[hh].b])
                    self.tt(self.ew(), QE[hh][:], Qt[:], EGB[hh][:], ALU.mult, [Qt.b, EGB[hh].b], [QE[hh].b])
                for srcv, dstv in ((Kt, KTOK), (INV[q][0], VTOK[0]), (INV[q][1], VTOK[1])):
                    ps = self.next_ps()
                    for ch in range(4):
                        cs_ = slice(ch * 128, (ch + 1) * 128)
                        self.tr(ps, ps[:, cs_], srcv[:, cs_], [srcv.b])
                    self.cp("scalar", dstv[:], ps[:], [ps.b], [dstv.b])
                for ch in range(4):
                    cs_ = slice(ch * 128, (ch + 1) * 128)
                    last = ch * 128 + 127
                    p = ch % 2
                    sm = SM[p]
                    scb = gsc.bufs[p]
                    ps_kq = self.next_ps()
                    self.mm(ps_kq, ps_kq[:, 0:128], Kt[:, cs_], Kt[:, cs_], True, True, [Kt.b])
                    self.mm(ps_kq, ps_kq[:, 128:256], Kt[:, cs_], Qt[:, cs_], True, True, [Kt.b, Qt.b])
                    pw = PW[0]
                    for hh in range(2):
                        GB, BB = ING[q][hh], INB[q][hh]
                        g_c = gcol[:, q, hh, 0, ch:ch + 1]
                        b_c = gcol[:, q, hh, 1, ch:ch + 1]
                        eg = gsc[:, p, hh, 0:1]
                        beg = gsc[:, p, hh, 1:2]
                        egl = gsc[:, p, hh, 2:3]
                        self.act(eg, g_c, AF.Exp, [gcb], [scb])
                        self.tt("vector", beg, eg, b_c, ALU.mult, [scb, gcb], [scb])
                        self.act(egl, g_c, AF.Exp, [gcb, GB.b], [scb], scale=-1.0, bias=GB[:, last:last + 1])
                        o = hh * 384
                        self.ts("vector", sm[:, o:o + 128], KTOK[:, cs_], beg, None, ALU.mult, None, [KTOK.b, scb], sm.bufs)
                        self.ts("gpsimd", sm[:, o + 128:o + 256], VTOK[hh][:, cs_], b_c, None, ALU.mult, None, [VTOK[hh].b, gcb], sm.bufs)
                        self.ts("vector", sm[:, o + 256:o + 384], KTOK[:, cs_], egl, None, ALU.mult, None, [KTOK.b, scb], sm.bufs)
                        dd, at = DD[hh], AT[hh]
                        self.ts("vector", dd[:, 0:128], GB[:, cs_], g_c, None, ALU.subtract, None, [GB.b, gcb], [dd.b])
                        self.ts("gpsimd", dd[:, 384:512], dd[:, 0:128], 0.0, None, ALU.min, None, [dd.b], [dd.b])
                        self.act(dd[:, 128:256], dd[:, 384:512], AF.Exp, [dd.b], [dd.b])
                        self.ts("vector", dd[:, 384:512], dd[:, 0:128], -1.0, 0.0, ALU.mult, ALU.min, [dd.b], [dd.b])
                        self.act(dd[:, 256:384], dd[:, 384:512], AF.Exp, [dd.b], [dd.b])
                        self.tt("vector", at[:, 128:256], ps_kq[:, 0:128], dd[:, 128:256], ALU.mult, [ps_kq.b, dd.b], [at.b])
                        self.tt("gpsimd", at[:, 128:256], at[:, 128:256], BB[:, cs_], ALU.mult, [at.b, BB.b], [at.b])
                        self.tt("vector", pw[:, hh * 256 + 128:hh * 256 + 256], at[:, 128:256], negm[:, 0:128], ALU.mult,
                                [at.b, negm.b], [pw.b])
                        self.tt("vector", at[:, 256:384], ps_kq[:, 0:128], dd[:, 256:384], ALU.mult, [ps_kq.b, dd.b], [at.b])
                        self.stt("gpsimd", pw[:, hh * 256:hh * 256 + 128], at[:, 256:384], b_c, negm[:, 128:256], ALU.mult, ALU.mult,
                                 [at.b, gcb, negm.b], [pw.b])
                        self.tt("vector", at[:, 0:128], ps_kq[:, 128:256], dd[:, 128:256], ALU.mult, [ps_kq.b, dd.b], [at.b])
                        self.tt("gpsimd", at[:, 0:128], at[:, 0:128], mask2[:, 128:256], ALU.mult, [at.b, mask2.b], [at.b])
                        self.tt("gpsimd", Q[:, hh * 128:hh * 128 + 128], pw[:, hh * 256 + 128:hh * 256 + 256], ident[:], ALU.add,
                                [pw.b, ident.b], [Q.b])
                    self.neumann(PW, Q)
                    ps_uw = self.next_ps()
                    for hh in range(2):
                        o = hh * 384
                        self.mm(ps_uw, ps_uw[:, hh * 256:hh * 256 + 128], Q[:, hh * 128:hh * 128 + 128], sm[:, o + 128:o + 256], True, True,
                                [Q.b] + sm.bufs)
                        self.mm(ps_uw, ps_uw[:, hh * 256 + 128:hh * 256 + 256], sm[:, o:o + 128], Q[:, hh * 128:hh * 128 + 128], True, True,
                                [Q.b] + sm.bufs)
                    self.cp("vector", UW[:], ps_uw[:], [ps_uw.b], [UW.b])
                    ps_vn = self.next_ps()
                    for hh in range(2):
                        self.mm(ps_vn, ps_vn[:, hh * 128:hh * 128 + 128], UW[:, hh * 256 + 128:hh * 256 + 256], S[:, hh * 128:hh * 128 + 128],
                                True, True, [UW.b, S.b])
                    for hh in range(2):
                        self.tt("vector", VN[:, hh * 128:hh * 128 + 128], UW[:, hh * 256:hh * 256 + 128], ps_vn[:, hh * 128:hh * 128 + 128],
                                ALU.subtract, [UW.b, ps_vn.b], [VN.b])
                    ps_o = self.next_ps()
                    for hh in range(2):
                        self.mm(ps_o, ps_o[:, hh * 128:hh * 128 + 128], QE[hh][:, cs_], S[:, hh * 128:hh * 128 + 128], True, False,
                                [QE[hh].b, S.b])
                        self.mm(ps_o, ps_o[:, hh * 128:hh * 128 + 128], AT[hh][:, 0:128], VN[:, hh * 128:hh * 128 + 128], False, True,
                                [AT[hh].b, VN.b])
                    for hh in range(2):
                        self.cp("scalar", OTOK[hh][:, cs_], ps_o[:, hh * 128:hh * 128 + 128], [ps_o.b], [OTOK[hh].b])
                    ps_s = self.next_ps()
                    for hh in range(2):
                        o = hh * 384
                        self.mm(ps_s, ps_s[:, hh * 128:hh * 128 + 128], sm[:, o + 256:o + 384], VN[:, hh * 128:hh * 128 + 128], True, True,
                                sm.bufs + [VN.b])
                    for hh in range(2):
                        self.stt("vector", S[:, hh * 128:hh * 128 + 128], S[:, hh * 128:hh * 128 + 128], EGB[hh][:, last:last + 1],
                                 ps_s[:, hh * 128:hh * 128 + 128], ALU.mult, ALU.add, [S.b, EGB[hh].b, ps_s.b], [S.b])
                for hh in range(2):
                    h = 2 * kh + hh
                    O = OTOK[hh]
                    SQ, ms4 = PT[0], PT[1]
                    O3 = O[:].rearrange("p (g v) -> p g v", v=128)
                    S3 = SQ[:].rearrange("p (g v) -> p g v", v=128)
                    self.tt("gpsimd", SQ[:], O[:], O[:], ALU.mult, [O.b], [SQ.b])
                    P.op("vector", lambda e, o_=ms4[:, 0:4], i_=S3: e.reduce_sum(out=o_, in_=i_, axis=mybir.AxisListType.X),
                         reads=[SQ.b], writes=[ms4.b])
                    self.rsqrt(ms4[:, 0:4], ms4[:, 0:4], NORM_EPS, [ms4.b], [ms4.b], mul=1.0 / 128)
                    self.tt("vector", S3, O3, ms4[:, 0:4].unsqueeze(2).to_broadcast([128, 4, 128]), ALU.mult, [O.b, ms4.b], [SQ.b])
                    ps_t = self.next_ps()
                    for ch in range(4):
                        cs_ = slice(ch * 128, (ch + 1) * 128)
                        self.tr(ps_t, ps_t[:, cs_], SQ[:, cs_], [SQ.b])
                    OF = PT[2]
                    self.act(OF[:], ps_t[:], AF.Copy, [ps_t.b, nw.b], [OF.b], scale=nw[:, 0])
                    og = OGo[hh]
                    self.tt("vector", og[:, 0, :], OF[:], INZ[q][hh][:], ALU.mult, [OF.b, INZ[q][hh].b], [og.b])
                    self.P.dma(self.gd_og[h, :, t0:t0 + TT], og[:, 0, :], reads=[og.b], writes=[self.dbuf("gdOG", h, ti)], key=("st", og.b))

    def gdn_phase3(self, L, xin, xin_name, xout, xout_name):
        j = L // 2
        cv = self.cv
        X = self.f32(0, 16)
        OG = self.bf(16, 32)
        w_o = self.wspecs[f"gout{j}"]
        for ti in range(self.NT):
            t0 = ti * TT
            self.load_x(X, xin, xin_name, ti)
            self.P.group_begin()
            for c in range(32):
                self.P.dma(OG[:, c, :], self.gd_og[c, :, t0:t0 + TT], reads=[self.dbuf("gdOG", c, ti)], writes=[OG.bufs[c]],
                           key="ldog")
            self.P.group_end()
            for n in range(DC):
                ps = self.linear(w_o, n, lambda kc: (OG[:, kc, :], [OG.bufs[kc]]))
                self.stt("vector", X[:, n, :], X[:, n, :], ALPHA, ps[:], ALU.mult, ALU.add, [ps.b, X.bufs[n]], [X.bufs[n]])
            self.layernorm(X, cv[f"mlg{L}"], cv[f"mlb{L}"])
            self.store_x(X, xout, xout_name, ti)


class CV:
    def __init__(self, t, off, n):
        self.t, self.off, self.n = t, off, n
        self.b = t.b

    def __getitem__(self, k):
        assert isinstance(k, tuple) and len(k) == 2
        ps, cs = k
        if isinstance(cs, slice):
            a = 0 if cs.start is None else cs.start
            b = self.n if cs.stop is None else cs.stop
            return self.t[ps, self.off + a:self.off + b]
        return self.t[ps, self.off + cs:self.off + cs + 1]


def fm(v):
    v = np.asarray(v, np.float32).reshape(-1)
    return np.ascontiguousarray(v.reshape(-1, 128).T)


def const_list(plan, inp):
    out = []

    def add(name, ncol, f):
        out.append((name, ncol, (f() if inp is not None else None)))
    seen = set()
    for kind, L in plan:
        j = L // 2
        if kind == "ffn":
            add(f"fcw{L}", 3 * 2 * FC, lambda: np.concatenate([fm(inp["ffn_conv_w"][L, k]) for k in range(3)], axis=1))
            add(f"fcb{L}", 2 * FC, lambda: fm(inp["ffn_conv_b"][L]))
            add(f"flg{L}", DC, lambda: fm(inp["ln_ffn_g"][L]))
            add(f"flb{L}", DC, lambda: fm(inp["ln_ffn_b"][L]))
        else:
            add(f"mlg{L}", DC, lambda: fm(inp["ln_mix_g"][L]))
            add(f"mlb{L}", DC, lambda: fm(inp["ln_mix_b"][L]))
        if kind == "rwkv" and ("rw", j) not in seen:
            seen.add(("rw", j))
            add(f"rmu{j}", 6 * DC, lambda: np.concatenate([fm(inp["rw_mu"][j, k]) for k in range(6)], axis=1))
            add(f"rw0{j}", DC, lambda: fm(inp["rw_w0"][j]))
            add(f"ra0{j}", DC, lambda: fm(inp["rw_a0"][j]))
            if j > 0:
                add(f"rv0{j}", DC, lambda: fm(inp["rw_v0"][j - 1]))
            add(f"rkk{j}", DC, lambda: fm(inp["rw_k_k"][j]))
            add(f"rka{j}", DC, lambda: fm(inp["rw_k_a"][j]))
            add(f"romka{j}", DC, lambda: np.zeros((128, DC), np.float32))
            add(f"rrk{j}", DC, lambda: fm(inp["rw_r_k"][j]))
            add(f"rlg{j}", DC, lambda: fm(inp["rw_lnx_g"][j]))
            add(f"rlb{j}", DC, lambda: fm(inp["rw_lnx_b"][j]))
        if kind == "gdn" and ("gd", j) not in seen:
            seen.add(("gd", j))
            add(f"gcw{j}", 4 * 64, lambda: np.concatenate([fm(inp["gdn_conv_w"][j, k]) for k in range(4)], axis=1))
            add(f"gnw{j}", 1, lambda: np.asarray(inp["gdn_norm_w"][j], np.float32).reshape(128, 1))

            def padv(v):
                a = np.zeros((128, 1), np.float32)
                a[32:64, 0] = np.asarray(v, np.float32)
                return a
            add(f"gal{j}", 1, lambda: padv(inp["gdn_a_log"][j]))
            add(f"gdt{j}", 1, lambda: padv(inp["gdn_dt_bias"][j]))
            add(f"gnea{j}", 1, lambda: np.zeros((128, 1), np.float32))
    return out


def prep_w(W):
    K, N = W.shape
    KC, NCH = (K + 127) // 128, (N + 127) // 128
    Wp = np.zeros((KC * 128, NCH * 128), np.float32)
    Wp[:K, :N] = W
    return np.ascontiguousarray(Wp.reshape(KC, 128, NCH, 128).transpose(2, 1, 0, 3).reshape(NCH, 128, KC * 128))


def weight_list(plan, inp):
    out = []

    def add(name, K, N, f):
        out.append((name, K, N, (prep_w(f()) if inp is not None else None)))
    for kind, L in plan:
        j = L // 2
        if kind == "ffn":
            add(f"up{L}", D, 2 * DFF, lambda: inp["ffn_w_up"][L])
            add(f"dn{L}", DFF, D, lambda: inp["ffn_w_down"][L])
        elif kind == "rwkv":
            add(f"rwr{j}", D, D, lambda: inp["rw_w_r"][j])
            add(f"rwk{j}", D, D, lambda: inp["rw_w_k"][j])
            add(f"rwv{j}", D, D, lambda: inp["rw_w_v"][j])
            add(f"rwo{j}", D, D, lambda: inp["rw_w_o"][j])
            add(f"rw1{j}", D, 96, lambda: inp["rw_w1"][j])
            add(f"rw2{j}", 96, D, lambda: inp["rw_w2"][j])
            add(f"ra1{j}", D, 96, lambda: inp["rw_a1"][j])
            add(f"ra2{j}", 96, D, lambda: inp["rw_a2"][j])
            add(f"rg1{j}", D, 256, lambda: inp["rw_g1"][j])
            add(f"rg2{j}", 256, D, lambda: inp["rw_g2"][j])
            if j > 0:
                add(f"rv1{j}", D, 64, lambda: inp["rw_v1"][j - 1])
                add(f"rv2{j}", 64, D, lambda: inp["rw_v2"][j - 1])
        elif kind == "gdn":
            add(f"gin{j}", D, 12352, lambda: inp["gdn_w_in"][j])
            add(f"gout{j}", 4096, D, lambda: inp["gdn_w_out"][j])
    return out


def build(T, plan):
    B = Builder(T, plan)
    nc, P = B.nc, B.P
    x_in = B.din("xT", [DC, 128, T])
    out = nc.dram_tensor("outT", [DC, 128, T], F32, kind="ExternalOutput").ap()
    xs = [B.dscratch("xs0", [DC, 128, T]), B.dscratch("xs1", [DC, 128, T])]
    cl = const_list(plan, None)
    ntot = sum(n for _, n, _ in cl)
    cst_in = B.din("consts", [128, ntot])
    cst = P.tile("consts_sb", [128, ntot], F32)
    P.dma(cst[:], cst_in, writes=[cst.b], key="consts")
    B.cv = {}
    off = 0
    for name, n, _ in cl:
        B.cv[name] = CV(cst, off, n)
        off += n
    for name, K, N, _ in weight_list(plan, None):
        KC, NCH = (K + 127) // 128, (N + 127) // 128
        B.wspec(name, B.din("w_" + name, [NCH, 128, KC * 128]), K, N)

    B.setup_common()
    for w in B.wspecs.values():
        B.convert(w)

    cur, cur_name = x_in, "xin"
    nxt_i = 0
    for si, (kind, L) in enumerate(plan):
        last = si == len(plan) - 1
        dst, dst_name = (out, "out") if last else (xs[nxt_i], f"xs{nxt_i}")
        if kind == "ffn":
            B.ffn_stage(L, cur, cur_name, dst, dst_name)
        elif kind == "rwkv":
            B.rwkv_stage(L, cur, cur_name, dst, dst_name)
        elif kind == "gdn":
            B.gdn_stage(L, cur, cur_name, dst, dst_name)
        cur, cur_name = dst, dst_name
        nxt_i ^= 1
    P.emit()
    return nc, B


def host_inputs(plan, inp, xb):
    m = {}
    T = xb.shape[0]
    m["xT"] = np.ascontiguousarray(xb.T.reshape(DC, 128, T))
    cl = const_list(plan, inp)
    m["consts"] = np.ascontiguousarray(np.concatenate([a for _, _, a in cl], axis=1).astype(np.float32))
    for name, K, N, a in weight_list(plan, inp):
        m["w_" + name] = a
    return m


FULL_PLAN = [("rwkv", 0), ("ffn", 0), ("gdn", 1), ("ffn", 1), ("rwkv", 2), ("ffn", 2), ("gdn", 3), ("ffn", 3)]


def kernel(**inputs):
    inp = {k: np.asarray(v) for k, v in inputs.items()}
    x = inp["x"]
    Bn, T, _ = x.shape
    nc, B = build(T, FULL_PLAN)
    shared = host_inputs(FULL_PLAN, inp, x[0])
    in_maps = []
    for i in range(Bn):
        m = dict(shared)
        m["xT"] = np.ascontiguousarray(x[i].T.reshape(DC, 128, T))
        in_maps.append(m)
    res = run_bass_kernel_spmd(nc, in_maps, core_ids=list(range(Bn)))
    outs = [np.asarray(r["outT"]).reshape(D, T).T for r in res.results]
    return np.ascontiguousarray(np.stack(outs, axis=0).astype(np.float32))
```

```python
import numpy as np
import concourse.bass as bass
import concourse.mybir as mybir
from concourse.bass_utils import run_bass_kernel_spmd

F32 = mybir.dt.float32
BF16 = mybir.dt.bfloat16
ALU = mybir.AluOpType
AF = mybir.ActivationFunctionType

D = 2048
DC = 16
SEQ = 4096
DEPTH = 4
ALPHA = (2 * DEPTH) ** 0.25
LN_EPS = 1e-5
NORM_EPS = 1e-6
DFF = 5504
FC = 43
TT = 512

ENGS = ["tensor", "vector", "scalar", "gpsimd", "sync"]
SAME_ENGINE_SYNC = True
DEBUG_TAGS = False


class Buf:
    __slots__ = ("name", "lw", "rd")

    def __init__(self, name):
        self.name = name
        self.lw = None
        self.rd = []


class Op:
    __slots__ = ("eng", "fn", "deps", "idx", "sig", "sigcnt", "is_dma", "dma_sem", "dma_cnt", "tag")


class Tile:
    def __init__(self, prog, name, shape, dtype, nb=1, psum=False):
        nc = prog.nc
        if psum:
            self.h = nc.alloc_psum_tensor(name, list(shape), dtype)
        else:
            self.h = nc.alloc_sbuf_tensor(name, list(shape), dtype)
        self.bufs = [Buf(f"{name}.{i}") for i in range(nb)]
        self.b = self.bufs[0]
        self.shape = shape

    def __getitem__(self, k):
        return self.h[k]


class Prog:
    def __init__(self, nc):
        self.nc = nc
        self.streams = {e: [] for e in ENGS}
        self.esem = {}
        self.dsem = {}
        self.dcnt = {}
        self.n_ops = 0
        self._grp = None

    def group_begin(self):
        assert self._grp is None
        self._grp = []

    def group_end(self):
        members = set(o for o, _ in self._grp)
        for o, key in self._grp:
            o.dma_cnt = self.dcnt[key]
            o.deps = o.deps - members
        self._grp = None

    def tile(self, name, shape, dtype, nb=1):
        return Tile(self, name, shape, dtype, nb)

    def psum(self, name, shape, dtype=F32):
        return Tile(self, name, shape, dtype, 1, psum=True)

    def _track(self, o, reads, writes):
        deps = set()
        for b in reads:
            if b.lw is not None:
                deps.add(b.lw)
        for b in writes:
            if b.lw is not None:
                deps.add(b.lw)
            for r in b.rd:
                deps.add(r)
        deps.discard(o)
        o.deps = deps
        for b in reads:
            b.rd.append(o)
        for b in writes:
            b.lw = o
            b.rd = []

    def op(self, eng, fn, reads=(), writes=()):
        o = Op()
        o.eng = eng
        o.fn = fn
        o.sig = False
        o.sigcnt = 0
        o.is_dma = False
        o.idx = len(self.streams[eng])
        if DEBUG_TAGS:
            import sys as _s
            f = _s._getframe(1)
            tg = []
            for _ in range(4):
                if f is None:
                    break
                tg.append(str(f.f_lineno))
                f = f.f_back
            o.tag = "/".join(tg)
        self._track(o, reads, writes)
        self.streams[eng].append(o)
        self.n_ops += 1
        return o

    def dma(self, out, in_, reads=(), writes=(), key=None, eng="sync", slow=False):
        o = Op()
        o.eng = eng
        if slow:
            o.fn = lambda e, out=out, in_=in_: e.dma_start(out=out, in_=in_, allow_slow_non_contiguous=True)
        else:
            o.fn = lambda e, out=out, in_=in_: e.dma_start(out=out, in_=in_)
        o.sig = False
        o.sigcnt = 0
        o.is_dma = True
        o.idx = len(self.streams[eng])
        if key not in self.dsem:
            self.dsem[key] = self.nc.alloc_semaphore(f"d{len(self.dsem)}")
            self.dcnt[key] = 0
        self.dcnt[key] += 16
        o.dma_sem = self.dsem[key]
        o.dma_cnt = self.dcnt[key]
        if self._grp is not None:
            self._grp.append((o, key))
        self._track(o, reads, writes)
        self.streams[eng].append(o)
        self.n_ops += 1
        return o

    def emit(self):
        nc = self.nc
        for e in ENGS:
            self.esem[e] = nc.alloc_semaphore(f"e_{e}")
        for e in ENGS:
            for o in self.streams[e]:
                for d in o.deps:
                    if not d.is_dma:
                        if d.eng == o.eng and (d.eng == "tensor" or not SAME_ENGINE_SYNC):
                            continue
                        d.sig = True
        for e in ENGS:
            c = 0
            for o in self.streams[e]:
                if o.sig:
                    c += 1
                o.sigcnt = c
        prog = self
        with nc.Block() as block:
            def make(engname):
                def body(e):
                    waited = {}
                    for o in prog.streams[engname]:
                        need = {}
                        for d in o.deps:
                            if d.is_dma:
                                k, v = d.dma_sem, d.dma_cnt
                            else:
                                if d.eng == engname and (engname == "tensor" or not SAME_ENGINE_SYNC):
                                    continue
                                k, v = prog.esem[d.eng], d.sigcnt
                            if need.get(k, 0) < v:
                                need[k] = v
                        for k, v in need.items():
                            if waited.get(k, 0) < v:
                                e.wait_ge(k, v)
                                waited[k] = v
                        ins = o.fn(e)
                        if DEBUG_TAGS and not o.is_dma:
                            ins.annotate(o.tag)
                        if o.is_dma:
                            ins.then_inc(o.dma_sem, 16)
                        elif o.sig:
                            ins.then_inc(prog.esem[engname], 1)
                    if engname == "sync":
                        for k, sem in prog.dsem.items():
                            if waited.get(sem, 0) < prog.dcnt[k]:
                                e.wait_ge(sem, prog.dcnt[k])
                return body
            block.tensor(make("tensor"))
            block.vector(make("vector"))
            block.scalar(make("scalar"))
            block.gpsimd(make("gpsimd"))
            block.sync(make("sync"))


class WSpec:
    def __init__(self, nc, name, src, K, N):
        self.name = name
        self.src = src
        self.K = K
        self.N = N
        self.KC = (K + 127) // 128
        self.NCH = (N + 127) // 128
        self.dst = nc.dram_tensor(f"wb_{name}", [self.NCH, 128, self.KC * 128], BF16, kind="Internal").ap()
        self.buf = Buf(f"wb_{name}")
        self.pbuf = {}
        for n in range(self.NCH):
            for c0 in range(0, self.KC * 128, 1024):
                self.pbuf[(n, c0)] = Buf(f"wb_{name}_{n}_{c0}")


class V:
    def __init__(self, ap, bufs):
        self.ap = ap
        self.bufs = list(bufs)
        self.b = self.bufs[0]

    def __getitem__(self, k):
        return self.ap[k]


NS = 82
RW_EXPM = float(np.exp(-0.5))


class Builder:
    def __init__(self, T, plan):
        self.T = T
        self.NT = T // TT
        self.plan = plan
        nc = bass.Bass("TRN2", target_bir_lowering=False)
        self.nc = nc
        self.P = Prog(nc)
        self.inputs = {}
        self.wspecs = {}
        self.cv_i = 0
        self._dbufs = {}
        self.rr = 0
        self.ps_i = 0
        self.ws_i = 0
        self.pending = []

    def din(self, name, shape):
        ap = self.nc.dram_tensor(name, list(shape), F32, kind="ExternalInput").ap()
        self.inputs[name] = ap
        return ap

    def dscratch(self, name, shape, dtype=F32):
        return self.nc.dram_tensor(name, list(shape), dtype, kind="Internal").ap()

    def wspec(self, name, src, K, N):
        w = WSpec(self.nc, name, src, K, N)
        self.wspecs[name] = w
        return w

    def dbuf(self, *key):
        if key not in self._dbufs:
            self._dbufs[key] = Buf("dram")
        return self._dbufs[key]

    def setup_common(self):
        P = self.P
        self.AR = P.tile("arena", [128, NS, 512], F32, nb=NS)
        self.ps = [P.psum(f"ps{i}", [128, 512]) for i in range(8)]
        self.wslots = [P.tile(f"wsl{i}", [128, 2048], BF16) for i in range(5)]
        self.ones = P.tile("ones", [128, 128], F32)
        self.ident = P.tile("ident", [128, 128], F32)
        self.bones = P.tile("bones", [128, 128], F32)
        self.mask2 = P.tile("mask2", [128, 256], F32)
        self.maskT = P.tile("maskT", [128, 256], F32)
        self.xprev = P.tile("xprev", [128, DC], F32)
        g = "gpsimd"
        P.op(g, lambda e: e.memset(self.ones[:], 1.0), writes=[self.ones.b])
        P.op(g, lambda e: e.memset(self.bones[:], 0.0), writes=[self.bones.b])
        P.op(g, lambda e: e.memset(self.bones[0:64, 0:64], 1.0), writes=[self.bones.b])
        P.op(g, lambda e: e.memset(self.bones[64:128, 64:128], 1.0), writes=[self.bones.b])
        P.op(g, lambda e: e.affine_select(out=self.ident[:], in_=self.ones[:], pattern=[[-1, 128]], compare_op=ALU.is_equal,
                                          fill=0.0, base=0, channel_multiplier=1), reads=[self.ones.b], writes=[self.ident.b])
        P.op(g, lambda e: e.affine_select(out=self.mask2[:, 0:128], in_=self.ones[:], pattern=[[1, 128]], compare_op=ALU.is_gt,
                                          fill=0.0, base=0, channel_multiplier=-1), reads=[self.ones.b], writes=[self.mask2.b])
        P.op(g, lambda e: e.affine_select(out=self.mask2[:, 128:256], in_=self.ones[:], pattern=[[1, 128]], compare_op=ALU.is_ge,
                                          fill=0.0, base=0, channel_multiplier=-1), reads=[self.ones.b], writes=[self.mask2.b])
        P.op(g, lambda e: e.affine_select(out=self.maskT[:, 0:128], in_=self.ones[:], pattern=[[-1, 128]], compare_op=ALU.is_gt,
                                          fill=0.0, base=0, channel_multiplier=1), reads=[self.ones.b], writes=[self.maskT.b])
        P.op(g, lambda e: e.affine_select(out=self.maskT[:, 128:256], in_=self.ones[:], pattern=[[-1, 128]], compare_op=ALU.is_ge,
                                          fill=0.0, base=0, channel_multiplier=1), reads=[self.ones.b], writes=[self.maskT.b])

    def f32(self, s0, n):
        return V(self.AR[:, s0:s0 + n, :], self.AR.bufs[s0:s0 + n])

    def sl(self, s):
        return V(self.AR[:, s, :], [self.AR.bufs[s]])

    def flat(self, s0, n):
        return V(self.AR[:, s0:s0 + n, :].rearrange("p s t -> p (s t)"), self.AR.bufs[s0:s0 + n])

    def bf(self, s0, nch):
        ns = (nch + 1) // 2
        ap = self.AR[:, s0:s0 + ns, :].bitcast(BF16).rearrange("p s (h t) -> p (s h) t", h=2)
        return V(ap, [self.AR.bufs[s0 + c // 2] for c in range(2 * ns)])

    def next_ps(self):
        t = self.ps[self.ps_i % 8]
        self.ps_i += 1
        return t

    def ew(self):
        self.rr += 1
        return ["vector", "gpsimd"][self.rr % 2]

    def mm(self, ps, out, lhsT, rhs, start, stop, reads):
        self.P.op("tensor", lambda e: e.matmul(out, lhsT=lhsT, rhs=rhs, start=start, stop=stop), reads=reads, writes=[ps.b])

    def tr(self, ps, out, in_, reads):
        K = in_.shape[0]
        idn = self.ident
        self.P.op("tensor", lambda e: e.transpose(out, in_, idn[0:K, 0:K]), reads=list(reads) + [idn.b], writes=[ps.b])

    def act(self, out, in_, func, reads, writes, bias=None, scale=None):
        kw = {}
        if bias is not None:
            kw["bias"] = bias
        if scale is not None:
            kw["scale"] = scale
        self.P.op("scalar", lambda e: e.activation(out=out, in_=in_, func=func, **kw), reads=reads, writes=writes)

    def tt(self, eng, out, in0, in1, op, reads, writes):
        self.P.op(eng, lambda e: e.tensor_tensor(out=out, in0=in0, in1=in1, op=op), reads=reads, writes=writes)

    def ts(self, eng, out, in0, s1, s2, op0, op1, reads, writes):
        if s2 is None:
            self.P.op(eng, lambda e: e.tensor_scalar(out=out, in0=in0, scalar1=s1, scalar2=None, op0=op0), reads=reads, writes=writes)
        else:
            self.P.op(eng, lambda e: e.tensor_scalar(out=out, in0=in0, scalar1=s1, scalar2=s2, op0=op0, op1=op1), reads=reads, writes=writes)

    def stt(self, eng, out, in0, scalar, in1, op0, op1, reads, writes):
        eng = "vector"
        self.P.op(eng, lambda e: e.scalar_tensor_tensor(out=out, in0=in0, scalar=scalar, in1=in1, op0=op0, op1=op1), reads=reads, writes=writes)

    def cp(self, eng, out, in_, reads, writes):
        if eng == "scalar":
            self.P.op(eng, lambda e: e.copy(out=out, in_=in_), reads=reads, writes=writes)
        else:
            self.P.op(eng, lambda e: e.tensor_copy(out=out, in_=in_), reads=reads, writes=writes)

    def rsqrt(self, out, in_, eps, reads, writes, mul=1.0):
        self.act(out, in_, AF.Ln, reads, writes, bias=float(eps), scale=float(mul))
        self.act(out, out, AF.Exp, writes, writes, scale=-0.5)

    def cv_init(self):
        self.cv_todo = []
        for w in self.wspecs.values():
            R = w.KC * 128
            for n in range(w.NCH):
                for c0 in range(0, R, 1024):
                    self.cv_todo.append((w, n, c0, min(1024, R - c0)))
        self.cv_inflight = None
        self.cv_slots_f = [self.flat(54, 2), self.flat(56, 2), self.flat(58, 2), self.flat(60, 2)]
        self.cv_slots_b = [self.bf(62, 2), self.bf(63, 2), self.bf(64, 2), self.bf(65, 2)]

    def cv_pump(self, k=1):
        P = self.P
        for _ in range(k):
            if self.cv_inflight is not None:
                i, (w, n, c0, cw) = self.cv_inflight
                tf, tb = self.cv_slots_f[i], self.cv_slots_b[i]
                tbf = tb.ap.rearrange("p a t -> p (a t)")
                ce = ["gpsimd", "vector", "scalar"][self.cv_i % 3]
                self.cp(ce, tbf[:, 0:cw], tf[:, 0:cw], tf.bufs, [tb.b])
                P.dma(w.dst[n, :, c0:c0 + cw], tbf[:, 0:cw], reads=[tb.b], writes=[w.pbuf[(n, c0)]], key=("cvst", i))
                self.cv_inflight = None
            if self.cv_todo:
                step = self.cv_todo.pop(0)
                w, n, c0, cw = step
                i = self.cv_i % 4
                self.cv_i += 1
                tf = self.cv_slots_f[i]
                P.dma(tf[:, 0:cw], w.src[n, :, c0:c0 + cw], writes=tf.bufs, key=("cv", i))
                self.cv_inflight = (i, step)

    def cv_flush(self):
        if self.cv_inflight is not None:
            todo, self.cv_todo = self.cv_todo, []
            self.cv_pump(1)
            self.cv_todo = todo

    def cv_drain(self, names):
        names = set(names)
        while (self.cv_inflight is not None and self.cv_inflight[1][0].name in names) or \
                any(st[0].name in names for st in self.cv_todo):
            self.cv_pump(1)
        self.cv_flush()

    def wload(self, w, n, kc0=0, nkc=None):
        P = self.P
        if nkc is None:
            nkc = w.KC
        t = self.wslots[self.ws_i % len(self.wslots)]
        self.ws_i += 1
        lo, hi = kc0 * 128, (kc0 + nkc) * 128
        pcs = [w.pbuf[(n, c0)] for c0 in range(0, w.KC * 128, 1024) if c0 < hi and c0 + 1024 > lo]
        P.dma(t[:, 0:nkc * 128], w.dst[n, :, lo:hi], reads=pcs, writes=[t.b], key=t.b)
        self.flush_stores()
        return t

    def linear(self, w, n, rhs_fn, nk=None, M=128):
        ps = self.next_ps()
        KC = w.KC if nk is None else nk
        for k0 in range(0, KC, 16):
            nkc = min(16, KC - k0)
            wt = self.wload(w, n, k0, nkc)
            for kk in range(nkc):
                kc = k0 + kk
                rows = min(128, w.K - kc * 128)
                rap, rb = rhs_fn(kc)
                self.mm(ps, ps[0:M, :], wt[0:rows, kk * 128:kk * 128 + M], rap, kc == 0, kc == KC - 1, [wt.b] + list(rb))
        return ps

    def layernorm(self, S, g, b):
        P = self.P
        ps_s = self.next_ps()
        ps_q = self.next_ps()
        ones = self.ones
        sqs = [self.sl(58), self.sl(59)]
        mean, rstd, nmr = self.sl(60), self.sl(61), self.sl(62)
        tmps = [self.sl(63), self.sl(64)]
        for c in range(DC):
            sq = sqs[c % 2]
            self.act(sq[:], S[:, c, :], AF.Square, [S.bufs[c]], [sq.b])
            self.mm(ps_s, ps_s[:], ones[:], S[:, c, :], c == 0, c == DC - 1, [S.bufs[c], ones.b])
            self.mm(ps_q, ps_q[:], ones[:], sq[:], c == 0, c == DC - 1, [sq.b, ones.b])
        self.ts("vector", mean[:], ps_s[:], 1.0 / D, None, ALU.mult, None, [ps_s.b], [mean.b])
        self.tt("vector", nmr[:], mean[:], mean[:], ALU.mult, [mean.b], [nmr.b])
        self.stt("vector", rstd[:], ps_q[:], 1.0 / D, nmr[:], ALU.mult, ALU.subtract, [ps_q.b, nmr.b], [rstd.b])
        self.rsqrt(rstd[:], rstd[:], LN_EPS, [rstd.b], [rstd.b])
        self.stt("vector", nmr[:], mean[:], -1.0, rstd[:], ALU.mult, ALU.mult, [mean.b, rstd.b], [nmr.b])
        for c in range(DC):
            tmp = tmps[c % 2]
            eng = self.ew()
            self.tt(eng, tmp[:], S[:, c, :], rstd[:], ALU.mult, [S.bufs[c], rstd.b], [tmp.b])
            self.tt(eng, tmp[:], tmp[:], nmr[:], ALU.add, [tmp.b, nmr.b], [tmp.b])
            self.act(S[:, c, :], tmp[:], AF.Identity, [tmp.b, g.b, b.b], [S.bufs[c]], bias=b[:, c:c + 1], scale=g[:, c:c + 1])

    def load_x(self, X, src, sname, ti):
        t0 = ti * TT
        self.P.group_begin()
        for c in range(DC):
            self.P.dma(X[:, c, :], src[c, :, t0:t0 + TT], reads=[self.dbuf(sname, c, ti)], writes=[X.bufs[c]], key="ldx")
        self.P.group_end()

    def store_x(self, X, dst, dname, ti):
        t0 = ti * TT
        self.P.group_begin()
        for c in range(DC):
            self.P.dma(dst[c, :, t0:t0 + TT], X[:, c, :], reads=[X.bufs[c]], writes=[self.dbuf(dname, c, ti)], key="stx")
        self.P.group_end()

    def ld(self, dstv, src_ap, dkey, key=None):
        self.P.dma(dstv.ap if isinstance(dstv, V) else dstv, src_ap, reads=[self.dbuf(*dkey)], writes=dstv.bufs,
                   key=("ld", dstv.b) if key is None else key)

    def st(self, dst_ap, srcv, dkey, src_ap=None):
        ent = (dst_ap, srcv.ap if src_ap is None else src_ap, list(srcv.bufs), dkey, [b.lw for b in srcv.bufs], self.ws_i)
        self.pending.append(ent)

    def flush_stores(self, age=2):
        keep = []
        for ent in self.pending:
            dst_ap, sap, bufs, dkey, lws, born = ent
            if self.ws_i - born >= age:
                assert all(b.lw is lw for b, lw in zip(bufs, lws)), "deferred store source was overwritten"
                self.P.dma(dst_ap, sap, reads=bufs, writes=[self.dbuf(*dkey)], key=("st", bufs[0]))
            else:
                keep.append(ent)
        self.pending = keep

    def ffn_stage(self, L, xin, xin_name, xout, xout_name):
        P = self.P
        cv = self.cv
        X = self.f32(0, 16)
        Xb = self.bf(16, 16)
        f_act = self.bf(24, FC)
        hb = [[self.flat(46, 2), self.flat(48, 2)], [self.flat(50, 2), self.flat(52, 2)]]
        ob = [[self.sl(54), self.sl(55)], [self.sl(56), self.sl(57)]]
        w_up, w_dn = self.wspecs[f"up{L}"], self.wspecs[f"dn{L}"]
        cw, cb = cv[f"fcw{L}"], cv[f"fcb{L}"]
        NH = 2 * FC
        if not hasattr(self, "f_carry"):
            self.f_carry = P.tile("f_carry", [128, 2 * FC, 2], F32, nb=2 * FC)
        carry = self.f_carry
        for ti in range(self.NT):
            self.load_x(X, xin, xin_name, ti)
            for c in range(DC):
                self.cp(self.ew(), Xb[:, c, :], X[:, c, :], [X.bufs[c]], [Xb.bufs[c]])
            for j in range(FC):
                outs = []
                for half in (0, 1):
                    hbuf = hb[half][j % 2]
                    obuf = ob[half][j % 2]
                    n = j + half * FC
                    ps = self.linear(w_up, n, lambda kc: (Xb[:, kc, :], [Xb.bufs[kc]]))
                    car = carry.bufs[n]
                    if ti == 0:
                        P.op("gpsimd", lambda e, hbuf=hbuf: e.memset(hbuf[:, 0:2], 0.0), writes=hbuf.bufs)
                    else:
                        self.cp("gpsimd", hbuf[:, 0:2], carry[:, n, :], [car], hbuf.bufs)
                    self.cp("scalar", hbuf[:, 2:514], ps[:], [ps.b], hbuf.bufs)
                    self.cp("gpsimd", carry[:, n, :], hbuf[:, 512:514], hbuf.bufs, [car])
                    self.ts("vector", obuf[:], hbuf[:, 2:514], cw[:, 2 * NH + n], cb[:, n], ALU.mult, ALU.add,
                            hbuf.bufs + [cw.b], [obuf.b])
                    self.stt("vector", obuf[:], hbuf[:, 1:513], cw[:, NH + n], obuf[:], ALU.mult, ALU.add,
                             hbuf.bufs + [cw.b, obuf.b], [obuf.b])
                    self.stt("gpsimd", obuf[:], hbuf[:, 0:512], cw[:, n], obuf[:], ALU.mult, ALU.add,
                             hbuf.bufs + [cw.b, obuf.b], [obuf.b])
                    outs.append(obuf)
                og, ou = outs
                self.act(og[:], og[:], AF.Silu, [og.b], [og.b])
                self.tt("vector", f_act[:, j, :], og[:], ou[:], ALU.mult, [og.b, ou.b], [f_act.bufs[j]])
            for n in range(DC):
                ps = self.linear(w_dn, n, lambda kc: (f_act[:, kc, :], [f_act.bufs[kc]]))
                self.stt("vector", X[:, n, :], X[:, n, :], ALPHA, ps[:], ALU.mult, ALU.add, [ps.b, X.bufs[n]], [X.bufs[n]])
            self.layernorm(X, cv[f"flg{L}"], cv[f"flb{L}"])
            self.store_x(X, xout, xout_name, ti)

    def rwkv_setup_dram(self):
        if hasattr(self, "rw_d"):
            return
        T = self.T
        self.rw_d = {nm: self.dscratch("rw_" + nm, [DC, 128, T]) for nm in ("R", "KP", "V", "LD", "KK", "BV", "G", "VF")}
        self.rw_yg = self.dscratch("rw_YG", [DC, 128, T], BF16)

    def rwkv_stage(self, L, xin, xin_name, xout, xout_name):
        self.rwkv_setup_dram()
        self.rwkv_phase1(L, xin, xin_name)
        self.flush_stores(age=0)
        self.rwkv_phase2(L)
        self.cv_flush()
        self.rwkv_phase3(L, xin, xin_name, xout, xout_name)

    def rwkv_phase1(self, L, xin, xin_name):
        P = self.P
        j = L // 2
        cv = self.cv
        W = self.wspecs
        mu = cv[f"rmu{j}"]
        X = self.f32(0, 16)
        MIX = {"r": self.bf(16, 16), "k": self.bf(24, 16), "v": self.bf(32, 16), "t": self.bf(40, 16)}
        LOR = self.bf(48, 6)
        xx = [self.sl(51), self.sl(52)]
        outs = {nm: [self.sl(53 + 2 * i), self.sl(54 + 2 * i)] for i, nm in enumerate(("R", "KP", "V", "LD", "KK", "BV", "G"))}
        tmp = [self.sl(72 + i) for i in range(10)]
        xprev = self.xprev
        rd = self.rw_d
        omka = cv[f"romka{j}"]
        if not hasattr(self, "omka_done"):
            self.omka_done = set()
        if j not in self.omka_done:
            self.omka_done.add(j)
            ka = cv[f"rka{j}"]
            self.ts("vector", omka[:, :], ka[:, :], -1.0, 1.0, ALU.mult, ALU.add, [ka.b], [omka.b])

        def mix(m, mi, c):
            dst = MIX[m]
            self.stt(self.ew(), dst[:, c, :], xx[c % 2][:], mu[:, mi * DC + c], X[:, c, :], ALU.mult, ALU.add,
                     [xx[c % 2].b, X.bufs[c], mu.b], [dst.bufs[c]])

        for ti in range(self.NT):
            t0 = ti * TT
            self.load_x(X, xin, xin_name, ti)
            for c in range(DC):
                x_ = xx[c % 2]
                self.tt("vector", x_[:, 1:512], X[:, c, 0:511], X[:, c, 1:512], ALU.subtract, [X.bufs[c]], [x_.b])
                if ti == 0:
                    self.ts("vector", x_[:, 0:1], X[:, c, 0:1], -1.0, None, ALU.mult, None, [X.bufs[c]], [x_.b])
                else:
                    self.tt("vector", x_[:, 0:1], xprev[:, c:c + 1], X[:, c, 0:1], ALU.subtract, [X.bufs[c], xprev.b], [x_.b])
                mix("r", 0, c)
                mix("k", 2, c)
                mix("v", 3, c)
            for (mi, wname, lo, func, M) in ((1, f"rw1{j}", 0, AF.Tanh, 96), (4, f"ra1{j}", 1, AF.Copy, 96),
                                               (5, f"rg1{j}", 2, AF.Sigmoid, 128)) + (((3, f"rv1{j}", 4, AF.Copy, 64),) if j > 0 else ()):
                if mi != 3:
                    for c in range(DC):
                        x_ = xx[c % 2]
                        self.tt("vector", x_[:, 1:512], X[:, c, 0:511], X[:, c, 1:512], ALU.subtract, [X.bufs[c]], [x_.b])
                        if ti == 0:
                            self.ts("vector", x_[:, 0:1], X[:, c, 0:1], -1.0, None, ALU.mult, None, [X.bufs[c]], [x_.b])
                        else:
                            self.tt("vector", x_[:, 0:1], xprev[:, c:c + 1], X[:, c, 0:1], ALU.subtract, [X.bufs[c], xprev.b], [x_.b])
                        mix("t", mi, c)
                    src = MIX["t"]
                else:
                    src = MIX["v"]
                w = W[wname]
                for n in range(w.NCH):
                    ps = self.linear(w, n, lambda kc, src=src: (src[:, kc, :], [src.bufs[kc]]), M=M)
                    self.act(LOR[0:M, lo + n, :], ps[0:M, :], func, [ps.b], [LOR.bufs[lo + n]])
            for c in range(DC):
                self.cp("gpsimd", xprev[:, c:c + 1], X[:, c, 511:512], [X.bufs[c]], [xprev.b])
            for n in range(DC):
                q = n % 2
                o = {nm: outs[nm][q] for nm in outs}
                ps_r = self.linear(W[f"rwr{j}"], n, lambda kc: (MIX["r"][:, kc, :], [MIX["r"].bufs[kc]]))
                self.cp("scalar", o["R"][:], ps_r[:], [ps_r.b], [o["R"].b])
                self.st(rd["R"][n, :, t0:t0 + TT], o["R"], ("rwR", n, ti))
                ps_k = self.linear(W[f"rwk{j}"], n, lambda kc: (MIX["k"][:, kc, :], [MIX["k"].bufs[kc]]))
                kraw = tmp[0]
                self.cp("scalar", kraw[:], ps_k[:], [ps_k.b], [kraw.b])
                ps_v = self.linear(W[f"rwv{j}"], n, lambda kc: (MIX["v"][:, kc, :], [MIX["v"].bufs[kc]]))
                ps_w = self.linear(W[f"rw2{j}"], n, lambda kc: (LOR[0:96, 0, :], [LOR.bufs[0]]))
                self.act(o["LD"][:], ps_w[:], AF.Sigmoid, [ps_w.b, cv[f"rw0{j}"].b], [o["LD"].b], bias=cv[f"rw0{j}"][:, n])
                self.ts("gpsimd", o["LD"][:], o["LD"][:], -RW_EXPM, None, ALU.mult, None, [o["LD"].b], [o["LD"].b])
                self.st(rd["LD"][n, :, t0:t0 + TT], o["LD"], ("rwLD", n, ti))
                ps_a = self.linear(W[f"ra2{j}"], n, lambda kc: (LOR[0:96, 1, :], [LOR.bufs[1]]))
                A = tmp[1]
                self.act(A[:], ps_a[:], AF.Sigmoid, [ps_a.b, cv[f"ra0{j}"].b], [A.b], bias=cv[f"ra0{j}"][:, n])
                ps_g = self.linear(W[f"rg2{j}"], n, lambda kc: (LOR[:, 2 + kc, :], [LOR.bufs[2 + kc]]))
                self.cp("scalar", o["G"][:], ps_g[:], [ps_g.b], [o["G"].b])
                self.st(rd["G"][n, :, t0:t0 + TT], o["G"], ("rwG", n, ti))
                if j == 0:
                    self.cp("scalar", o["V"][:], ps_v[:], [ps_v.b], [o["V"].b])
                    self.st(rd["VF"][n, :, t0:t0 + TT], o["V"], ("rwVF", n, ti))
                else:
                    ps_l = self.linear(W[f"rv2{j}"], n, lambda kc: (LOR[0:64, 4, :], [LOR.bufs[4]]))
                    sg, vf, vr = tmp[2], tmp[3], tmp[4]
                    self.act(sg[:], ps_l[:], AF.Sigmoid, [ps_l.b, cv[f"rv0{j}"].b], [sg.b], bias=cv[f"rv0{j}"][:, n])
                    self.ld(vf, rd["VF"][n, :, t0:t0 + TT], ("rwVF", n, ti))
                    self.cp("scalar", vr[:], ps_v[:], [ps_v.b], [vr.b])
                    self.tt("vector", vf[:], vf[:], vr[:], ALU.subtract, [vf.b, vr.b], [vf.b])
                    self.tt("gpsimd", vf[:], vf[:], sg[:], ALU.mult, [vf.b, sg.b], [vf.b])
                    self.tt("vector", o["V"][:], vf[:], vr[:], ALU.add, [vf.b, vr.b], [o["V"].b])
                self.st(rd["V"][n, :, t0:t0 + TT], o["V"], ("rwV", n, ti))
                kku, sq, rn = tmp[5], tmp[6], tmp[7]
                self.ts("vector", kku[:], kraw[:], cv[f"rkk{j}"][:, n], None, ALU.mult, None, [kraw.b, cv[f"rkk{j}"].b], [kku.b])
                self.tt("gpsimd", sq[:], kku[:], kku[:], ALU.mult, [kku.b], [sq.b])
                ps_n = self.next_ps()
                self.mm(ps_n, ps_n[:], self.bones[:], sq[:], True, True, [self.bones.b, sq.b])
                self.cp("vector", rn[:], ps_n[:], [ps_n.b], [rn.b])
                self.rsqrt(rn[:], rn[:], NORM_EPS, [rn.b], [rn.b])
                self.tt("vector", o["KK"][:], kku[:], rn[:], ALU.mult, [kku.b, rn.b], [o["KK"].b])
                self.st(rd["KK"][n, :, t0:t0 + TT], o["KK"], ("rwKK", n, ti))
                self.tt("gpsimd", o["BV"][:], o["KK"][:], A[:], ALU.mult, [o["KK"].b, A.b], [o["BV"].b])
                self.st(rd["BV"][n, :, t0:t0 + TT], o["BV"], ("rwBV", n, ti))
                t1 = tmp[8]
                self.ts("vector", t1[:], A[:], cv[f"rka{j}"][:, n], omka[:, n], ALU.mult, ALU.add, [A.b, cv[f"rka{j}"].b, omka.b], [t1.b])
                self.tt("gpsimd", o["KP"][:], kraw[:], t1[:], ALU.mult, [kraw.b, t1.b], [o["KP"].b])
                self.st(rd["KP"][n, :, t0:t0 + TT], o["KP"], ("rwKP", n, ti))

    def rwkv_phase2(self, L):
        P = self.P
        j = L // 2
        cv = self.cv
        rd = self.rw_d
        names = ("R", "KP", "V", "LD", "KK", "BV", "G")
        IN = [{nm: self.sl(2 * i + q) for i, nm in enumerate(names)} for q in range(2)]
        CS, EP, EN, EPM, ED = (self.sl(14 + i) for i in range(5))
        ARt = self.flat(19, 2)
        BTt, KTt, BH, KH, TMP = (self.sl(21 + i) for i in range(5))
        VTOK, BHTOK, KHTOK, YTOK = (self.sl(26 + i) for i in range(4))
        ATS = [[self.flat(30, 2), self.flat(32, 2)], [self.flat(74, 2), self.flat(76, 2)]]
        PW = [self.sl(34), self.sl(35)]
        Q = self.sl(36)
        WU = self.sl(37)
        HBD = self.sl(38)
        PT = [self.sl(39 + i) for i in range(6)]
        YGo = [self.bf(45, 2), self.bf(78, 2)]
        ident = self.ident
        mask2, maskT = self.mask2, self.maskT
        m2b = mask2[:].unsqueeze(1).to_broadcast([128, 2, 256]) if False else None
        seq = [(c_, t_) for c_ in range(DC) for t_ in range(self.NT)]
        prefetch = (self.NT % 2 == 0)

        def issue(c_, t_):
            P.group_begin()
            for nm in names:
                self.ld(IN[t_ % 2][nm], rd[nm][c_, :, t_ * TT:(t_ + 1) * TT], ("rw" + nm, c_, t_), key=("ldin", t_ % 2))
            P.group_end()
        if prefetch:
            issue(*seq[0])
        for c in range(DC):
            P.op("gpsimd", lambda e: e.memset(HBD[:, 0:128], 0.0), writes=[HBD.b])
            for ti in range(self.NT):
                t0 = ti * TT
                I = IN[ti % 2]
                si_ = c * self.NT + ti
                if prefetch:
                    if si_ + 1 < len(seq):
                        issue(*seq[si_ + 1])
                else:
                    issue(c, ti)
                R, KP, Vv, LDv, KK, BV, G = (I[nm] for nm in names)
                for ch in range(4):
                    cs_ = slice(ch * 128, (ch + 1) * 128)
                    P.op("vector", lambda e, o_=CS[:, cs_], d1=LDv[:, cs_]: e.tensor_tensor_scan(
                        out=o_, data0=self.ones[:, 0:128], data1=d1, initial=0.0, op0=ALU.mult, op1=ALU.add),
                         reads=[LDv.b, self.ones.b], writes=[CS.b])
                self.act(EP[:], CS[:], AF.Exp, [CS.b], [EP.b])
                self.act(EN[:], CS[:], AF.Exp, [CS.b], [EN.b], scale=-1.0)
                self.tt("gpsimd", TMP[:], CS[:], LDv[:], ALU.subtract, [CS.b, LDv.b], [TMP.b])
                self.act(EPM[:], TMP[:], AF.Exp, [TMP.b], [EPM.b])
                for ch in range(4):
                    cs_ = slice(ch * 128, (ch + 1) * 128)
                    self.act(ED[:, cs_], CS[:, cs_], AF.Exp, [CS.b], [ED.b], scale=-1.0, bias=CS[:, ch * 128 + 127:ch * 128 + 128])
                    self.stt("vector", ARt[:, ch * 256:ch * 256 + 128], KK[:, cs_], -1.0, EPM[:, cs_], ALU.mult, ALU.mult,
                             [KK.b, EPM.b], ARt.bufs)
                    self.tt("gpsimd", ARt[:, ch * 256 + 128:ch * 256 + 256], R[:, cs_], EP[:, cs_], ALU.mult, [R.b, EP.b], ARt.bufs)
                self.tt("vector", BTt[:], BV[:], EN[:], ALU.mult, [BV.b, EN.b], [BTt.b])
                self.tt("gpsimd", KTt[:], KP[:], EN[:], ALU.mult, [KP.b, EN.b], [KTt.b])
                self.tt("vector", BH[:], BV[:], ED[:], ALU.mult, [BV.b, ED.b], [BH.b])
                self.tt("gpsimd", KH[:], KP[:], ED[:], ALU.mult, [KP.b, ED.b], [KH.b])
                for srcv, dstv in ((Vv, VTOK), (BH, BHTOK), (KH, KHTOK)):
                    ps = self.next_ps()
                    for ch in range(4):
                        cs_ = slice(ch * 128, (ch + 1) * 128)
                        self.tr(ps, ps[:, cs_], srcv[:, cs_], [srcv.b])
                    self.cp("scalar", dstv[:], ps[:], [ps.b], [dstv.b])
                for ch in range(4):
                    cs_ = slice(ch * 128, (ch + 1) * 128)
                    p = ch % 2
                    ATb, ATk = ATS[p]
                    self.cv_pump(1)
                    ps_b = self.next_ps()
                    ps_k = self.next_ps()
                    ps_a = self.next_ps()
                    for h in range(2):
                        hp = slice(64 * h, 64 * h + 64)
                        self.mm(ps_b, ps_b[:, h * 256:h * 256 + 256], BTt[hp, cs_], ARt[hp, ch * 256:ch * 256 + 256], True, True,
                                [BTt.b] + ARt.bufs)
                        self.mm(ps_k, ps_k[:, h * 256:h * 256 + 256], KTt[hp, cs_], ARt[hp, ch * 256:ch * 256 + 256], True, True,
                                [KTt.b] + ARt.bufs)
                        self.mm(ps_a, ps_a[:, h * 256:h * 256 + 128], ARt[hp, ch * 256:ch * 256 + 128], BTt[hp, cs_], True, True,
                                [BTt.b] + ARt.bufs)
                    for h in range(2):
                        self.tt("vector", ATb[:, h * 256:h * 256 + 256], ps_b[:, h * 256:h * 256 + 256], mask2[:], ALU.mult,
                                [ps_b.b, mask2.b], ATb.bufs)
                        self.tt("vector", ATk[:, h * 256:h * 256 + 256], ps_k[:, h * 256:h * 256 + 256], mask2[:], ALU.mult,
                                [ps_k.b, mask2.b], ATk.bufs)
                    pw = PW[0]
                    for h in range(2):
                        self.tt("vector", pw[:, h * 256:h * 256 + 128], ps_a[:, h * 256:h * 256 + 128], maskT[:, 0:128], ALU.mult,
                                [ps_a.b, maskT.b], [pw.b])
                        self.cp("gpsimd", pw[:, h * 256 + 128:h * 256 + 256], ATb[:, h * 256:h * 256 + 128], ATb.bufs, [pw.b])
                        self.tt("gpsimd", Q[:, h * 128:h * 128 + 128], ATb[:, h * 256:h * 256 + 128], ident[:], ALU.add,
                                ATb.bufs + [ident.b], [Q.b])
                    self.neumann(PW, Q)
                    ps_w = self.next_ps()
                    self.mm(ps_w, ps_w[:, 0:128], ARt[:, ch * 256:ch * 256 + 128], HBD[:, 0:128], True, False, ARt.bufs + [HBD.b])
                    for h in range(2):
                        self.mm(ps_w, ps_w[:, h * 64:h * 64 + 64], ATk[:, h * 256:h * 256 + 128], VTOK[:, ch * 128 + h * 64:ch * 128 + h * 64 + 64],
                                False, h == 1, ATk.bufs + [VTOK.b])
                    self.cp("vector", WU[:, 0:128], ps_w[:, 0:128], [ps_w.b], [WU.b])
                    ps_u = self.next_ps()
                    for h in range(2):
                        self.mm(ps_u, ps_u[:, h * 64:h * 64 + 64], Q[:, h * 128:h * 128 + 128], WU[:, h * 64:h * 64 + 64], True, True,
                                [Q.b, WU.b])
                    self.cp("vector", WU[:, 128:256], ps_u[:, 0:128], [ps_u.b], [WU.b])
                    ps_y = self.next_ps()
                    self.mm(ps_y, ps_y[:, 0:128], ARt[:, ch * 256 + 128:ch * 256 + 256], HBD[:, 0:128], True, False, ARt.bufs + [HBD.b])
                    for h in range(2):
                        self.mm(ps_y, ps_y[:, h * 64:h * 64 + 64], ATk[:, h * 256 + 128:h * 256 + 256],
                                VTOK[:, ch * 128 + h * 64:ch * 128 + h * 64 + 64], False, False, ATk.bufs + [VTOK.b])
                    for h in range(2):
                        self.mm(ps_y, ps_y[:, h * 64:h * 64 + 64], ATb[:, h * 256 + 128:h * 256 + 256], WU[:, 128 + h * 64:128 + h * 64 + 64],
                                False, h == 1, ATb.bufs + [WU.b])
                    self.cp("scalar", YTOK[:, cs_], ps_y[:, 0:128], [ps_y.b], [YTOK.b])
                    ps_h = self.next_ps()
                    self.mm(ps_h, ps_h[:, 0:128], BHTOK[:, cs_], WU[:, 128:256], True, False, [BHTOK.b, WU.b])
                    self.mm(ps_h, ps_h[:, 0:128], KHTOK[:, cs_], VTOK[:, cs_], False, True, [KHTOK.b, VTOK.b])
                    for h in range(2):
                        hp = slice(64 * h, 64 * h + 64)
                        self.stt("vector", HBD[hp, 64 * h:64 * h + 64], HBD[hp, 64 * h:64 * h + 64], EP[hp, ch * 128 + 127:ch * 128 + 128],
                                 ps_h[hp, 64 * h:64 * h + 64], ALU.mult, ALU.add, [HBD.b, EP.b, ps_h.b], [HBD.b])
                mean8, var8 = PT[0], PT[1]
                CEN, SQ = PT[2], PT[3]
                Y3 = YTOK[:].rearrange("p (g v) -> p g v", v=64)
                C3 = CEN[:].rearrange("p (g v) -> p g v", v=64)
                S3 = SQ[:].rearrange("p (g v) -> p g v", v=64)
                P.op("vector", lambda e, o_=mean8[:, 0:8], i_=Y3: e.reduce_sum(out=o_, in_=i_, axis=mybir.AxisListType.X), reads=[YTOK.b], writes=[mean8.b])
                self.ts("vector", mean8[:, 0:8], mean8[:, 0:8], 1.0 / 64, None, ALU.mult, None, [mean8.b], [mean8.b])
                self.tt("vector", C3, Y3, mean8[:, 0:8].unsqueeze(2).to_broadcast([128, 8, 64]), ALU.subtract, [YTOK.b, mean8.b], [CEN.b])
                self.tt("gpsimd", SQ[:], CEN[:], CEN[:], ALU.mult, [CEN.b], [SQ.b])
                P.op("vector", lambda e, o_=var8[:, 0:8], i_=S3: e.reduce_sum(out=o_, in_=i_, axis=mybir.AxisListType.X), reads=[SQ.b], writes=[var8.b])
                self.rsqrt(var8[:, 0:8], var8[:, 0:8], 64e-5, [var8.b], [var8.b], mul=1.0 / 64)
                self.tt("vector", C3, C3, var8[:, 0:8].unsqueeze(2).to_broadcast([128, 8, 64]), ALU.mult, [CEN.b, var8.b], [CEN.b])
                ps_t = self.next_ps()
                for ch in range(4):
                    cs_ = slice(ch * 128, (ch + 1) * 128)
                    self.tr(ps_t, ps_t[:, cs_], CEN[:, cs_], [CEN.b])
                YA = PT[4]
                self.act(YA[:], ps_t[:], AF.Identity, [ps_t.b, cv[f"rlg{j}"].b, cv[f"rlb{j}"].b], [YA.b],
                         bias=cv[f"rlb{j}"][:, c], scale=cv[f"rlg{j}"][:, c])
                RK = PT[5]
                self.stt("vector", RK[:], R[:], cv[f"rrk{j}"][:, c], KP[:], ALU.mult, ALU.mult, [R.b, KP.b, cv[f"rrk{j}"].b], [RK.b])
                ps_bn = self.next_ps()
                self.mm(ps_bn, ps_bn[:], self.bones[:], RK[:], True, True, [self.bones.b, RK.b])
                self.tt("vector", RK[:], ps_bn[:], Vv[:], ALU.mult, [ps_bn.b, Vv.b], [RK.b])
                self.tt("gpsimd", YA[:], YA[:], RK[:], ALU.add, [YA.b, RK.b], [YA.b])
                yg = YGo[ti % 2]
                self.tt("vector", yg[:, 0, :], YA[:], G[:], ALU.mult, [YA.b, G.b], [yg.b])
                self.P.dma(self.rw_yg[c, :, t0:t0 + TT], yg[:, 0, :], reads=[yg.b], writes=[self.dbuf("rwYG", c, ti)], key=("st", yg.b))

    def rwkv_phase3(self, L, xin, xin_name, xout, xout_name):
        j = L // 2
        cv = self.cv
        X = self.f32(0, 16)
        YG = self.bf(16, 16)
        w_o = self.wspecs[f"rwo{j}"]
        for ti in range(self.NT):
            t0 = ti * TT
            self.load_x(X, xin, xin_name, ti)
            self.P.group_begin()
            for c in range(DC):
                self.P.dma(YG[:, c, :], self.rw_yg[c, :, t0:t0 + TT], reads=[self.dbuf("rwYG", c, ti)], writes=[YG.bufs[c]],
                           key="ldyg")
            self.P.group_end()
            for n in range(DC):
                ps = self.linear(w_o, n, lambda kc: (YG[:, kc, :], [YG.bufs[kc]]))
                self.stt("vector", X[:, n, :], X[:, n, :], ALPHA, ps[:], ALU.mult, ALU.add, [ps.b, X.bufs[n]], [X.bufs[n]])
            self.layernorm(X, cv[f"mlg{L}"], cv[f"mlb{L}"])
            self.store_x(X, xout, xout_name, ti)

    def neumann(self, PW, Q):
        for lvl in range(1, 7):
            src = PW[(lvl - 1) % 2]
            dst = PW[lvl % 2]
            psp = self.next_ps()
            for h in range(2):
                Ah = src[:, h * 256:h * 256 + 128]
                Nh = src[:, h * 256 + 128:h * 256 + 256]
                self.mm(psp, psp[:, h * 256:h * 256 + 128], Nh, Ah, True, True, [src.b])
                if lvl < 6:
                    self.mm(psp, psp[:, h * 256 + 128:h * 256 + 256], Ah, Nh, True, True, [src.b])
            if lvl < 6:
                self.cp("scalar", dst[:], psp[:], [psp.b], [dst.b])
            else:
                for h in range(2):
                    self.cp("scalar", dst[:, h * 256:h * 256 + 128], psp[:, h * 256:h * 256 + 128], [psp.b], [dst.b])
            psq = self.next_ps()
            for h in range(2):
                self.mm(psq, psq[:, h * 128:h * 128 + 128], dst[:, h * 256:h * 256 + 128], Q[:, h * 128:h * 128 + 128],
                        True, True, [dst.b, Q.b])
            self.tt("vector", Q[:, 0:256], Q[:, 0:256], psq[:, 0:256], ALU.add, [Q.b, psq.b], [Q.b])

    def gdn_setup_dram(self):
        if hasattr(self, "gd_d"):
            return
        T = self.T
        self.gd_d = {"QN": self.dscratch("gd_QN", [16, 128, T]), "KN": self.dscratch("gd_KN", [16, 128, T]),
                     "VV": self.dscratch("gd_VV", [32, 128, T]), "ZS": self.dscratch("gd_ZS", [32, 128, T]),
                     "BG": self.dscratch("gd_BG", [64, T])}
        self.gd_og = self.dscratch("gd_OG", [32, 128, T], BF16)
        P = self.P
        self.g_carry = P.tile("g_carry", [128, 64, 3], F32, nb=64)
        self.g_col = P.tile("g_col", [128, 2, 2, 2, 4], F32, nb=2)
        self.g_sc = P.tile("g_sc", [128, 2, 2, 4], F32, nb=2)
        self.negm = P.tile("negm", [128, 256], F32)
        self.ts("vector", self.negm[:, 0:128], self.mask2[:, 0:128], -1.0, None, ALU.mult, None, [self.mask2.b], [self.negm.b])
        self.ts("vector", self.negm[:, 128:256], self.maskT[:, 0:128], -1.0, None, ALU.mult, None, [self.maskT.b], [self.negm.b])

    def gdn_stage(self, L, xin, xin_name, xout, xout_name):
        self.gdn_setup_dram()
        self.gdn_phase1(L, xin, xin_name)
        self.flush_stores(age=0)
        self.gdn_phase2(L)
        self.cv_flush()
        self.gdn_phase3(L, xin, xin_name, xout, xout_name)

    def gdn_phase1(self, L, xin, xin_name):
        P = self.P
        j = L // 2
        cv = self.cv
        w_in = self.wspecs[f"gin{j}"]
        cw = cv[f"gcw{j}"]
        gd = self.gd_d
        X = self.f32(0, 16)
        Xb = self.bf(16, 16)
        hb = [self.flat(24, 2), self.flat(26, 2)]
        ob = [self.sl(28), self.sl(29), self.sl(36), self.sl(37)]
        t1 = [self.sl(30), self.sl(31)]
        t2 = [self.sl(32), self.sl(33)]
        bgt = self.sl(34)
        et = self.sl(35)
        carry = self.g_carry
        nea = cv[f"gnea{j}"]
        self.act(nea[:, :], cv[f"gal{j}"][:, :], AF.Exp, [cv[f"gal{j}"].b], [nea.b])
        self.ts("vector", nea[:, :], nea[:, :], -1.0, None, ALU.mult, None, [nea.b], [nea.b])
        for ti in range(self.NT):
            t0 = ti * TT
            self.load_x(X, xin, xin_name, ti)
            for c in range(DC):
                self.cp(self.ew(), Xb[:, c, :], X[:, c, :], [X.bufs[c]], [Xb.bufs[c]])
            rhs = lambda kc: (Xb[:, kc, :], [Xb.bufs[kc]])
            for n in range(64):
                hbuf, obuf = hb[n % 2], ob[n % 4]
                ps = self.linear(w_in, n, rhs)
                car = carry.bufs[n]
                if ti == 0:
                    P.op("gpsimd", lambda e, o_=hbuf[:, 0:3]: e.memset(o_, 0.0), writes=hbuf.bufs)
                else:
                    self.cp("gpsimd", hbuf[:, 0:3], carry[:, n, :], [car], hbuf.bufs)
                self.cp("scalar", hbuf[:, 3:515], ps[:], [ps.b], hbuf.bufs)
                self.cp("gpsimd", carry[:, n, :], hbuf[:, 512:515], hbuf.bufs, [car])
                self.ts("vector", obuf[:], hbuf[:, 3:515], cw[:, 3 * 64 + n], None, ALU.mult, None, hbuf.bufs + [cw.b], [obuf.b])
                for k, eng in ((2, "gpsimd"), (1, "vector"), (0, "gpsimd")):
                    self.stt(eng, obuf[:], hbuf[:, k:k + 512], cw[:, k * 64 + n], obuf[:], ALU.mult, ALU.add,
                             hbuf.bufs + [cw.b, obuf.b], [obuf.b])
                self.act(obuf[:], obuf[:], AF.Silu, [obuf.b], [obuf.b])
                if n < 32:
                    sq, rn = t1[n % 2], t2[n % 2]
                    self.tt("gpsimd", sq[:], obuf[:], obuf[:], ALU.mult, [obuf.b], [sq.b])
                    ps_n = self.next_ps()
                    self.mm(ps_n, ps_n[:], self.ones[:], sq[:], True, True, [self.ones.b, sq.b])
                    self.cp("vector", rn[:], ps_n[:], [ps_n.b], [rn.b])
                    self.rsqrt(rn[:], rn[:], NORM_EPS, [rn.b], [rn.b])
                    sc = (128.0 ** -0.5) if n < 16 else 1.0
                    self.stt("vector", obuf[:], obuf[:], sc, rn[:], ALU.mult, ALU.mult, [obuf.b, rn.b], [obuf.b])
                    nm, idx = ("QN", n) if n < 16 else ("KN", n - 16)
                else:
                    nm, idx = "VV", n - 32
                self.st(gd[nm][idx, :, t0:t0 + TT], obuf, ("gd" + nm, idx, ti))
            for n in range(64, 96):
                obuf = ob[n % 4]
                ps = self.linear(w_in, n, rhs)
                self.act(obuf[:], ps[:], AF.Silu, [ps.b], [obuf.b])
                self.st(gd["ZS"][n - 64, :, t0:t0 + TT], obuf, ("gdZS", n - 64, ti))
            ps = self.linear(w_in, 96, rhs, M=64)
            self.act(bgt[0:32, :], ps[0:32, :], AF.Sigmoid, [ps.b], [bgt.b])
            self.act(et[32:64, :], ps[32:64, :], AF.Exp, [ps.b, cv[f"gdt{j}"].b], [et.b], bias=cv[f"gdt{j}"][32:64, 0])
            self.act(et[32:64, :], et[32:64, :], AF.Ln, [et.b], [et.b], bias=1.0)
            self.ts("vector", et[32:64, :], et[32:64, :], nea[32:64, 0], None, ALU.mult, None, [et.b, nea.b], [et.b])
            for ch in range(4):
                cs_ = slice(ch * 128, (ch + 1) * 128)
                P.op("vector", lambda e, o_=bgt[32:64, cs_], d0=self.ones[32:64, 0:128], d1=et[32:64, cs_]: e.tensor_tensor_scan(
                    out=o_, data0=d0, data1=d1, initial=0.0, op0=ALU.mult, op1=ALU.add), reads=[et.b, self.ones.b], writes=[bgt.b])
            self.P.dma(gd["BG"][0:64, t0:t0 + TT], bgt[0:64, :], reads=[bgt.b], writes=[self.dbuf("gdBG", ti)], key=("st", bgt.b))

    def gdn_phase2(self, L):
        P = self.P
        j = L // 2
        cv = self.cv
        gd = self.gd_d
        INQ = [self.sl(0), self.sl(1)]
        INK = [self.sl(2), self.sl(3)]
        INV = [[self.sl(4), self.sl(5)], [self.sl(6), self.sl(7)]]
        INZ = [[self.sl(8), self.sl(9)], [self.sl(10), self.sl(11)]]
        ING = [[self.sl(12), self.sl(13)], [self.sl(14), self.sl(15)]]
        INB = [[self.sl(16), self.sl(17)], [self.sl(18), self.sl(19)]]
        EGB = [self.sl(20), self.sl(21)]
        QE = [self.sl(22), self.sl(23)]
        KTOK = self.sl(24)
        VTOK = [self.sl(25), self.sl(26)]
        SM = [self.flat(27, 2), self.flat(29, 2)]
        DD = [self.sl(31), self.sl(32)]
        AT = [self.sl(33), self.sl(34)]
        PW = [self.sl(37), self.sl(38)]
        Q = self.sl(39)
        UW = self.sl(40)
        VN = self.sl(41)
        S = self.sl(42)
        OTOK = [self.sl(44), self.sl(45)]
        PT = [self.sl(46 + i) for i in range(5)]
        OGo = [self.bf(51, 2), self.bf(52, 2)]
        ident = self.ident
        mask2, maskT, negm = self.mask2, self.maskT, self.negm
        gcol, gsc = self.g_col, self.g_sc
        bg = gd["BG"]
        nw = cv[f"gnw{j}"]
        seq = [(k_, t_) for k_ in range(16) for t_ in range(self.NT)]
        prefetch = (self.NT % 2 == 0)

        def issue(kh_, ti_):
            t0_ = ti_ * TT
            q_ = ti_ % 2
            lk = ("ldin", q_)
            gcb_ = gcol.bufs[q_]
            P.group_begin()
            self.ld(INQ[q_], gd["QN"][kh_, :, t0_:t0_ + TT], ("gdQN", kh_, ti_), key=lk)
            self.ld(INK[q_], gd["KN"][kh_, :, t0_:t0_ + TT], ("gdKN", kh_, ti_), key=lk)
            for hh in range(2):
                h = 2 * kh_ + hh
                self.ld(INV[q_][hh], gd["VV"][h, :, t0_:t0_ + TT], ("gdVV", h, ti_), key=lk)
                self.ld(INZ[q_][hh], gd["ZS"][h, :, t0_:t0_ + TT], ("gdZS", h, ti_), key=lk)
                self.ld(ING[q_][hh], bg[32 + h:33 + h, t0_:t0_ + TT].partition_broadcast(128), ("gdBG", ti_), key=lk)
                self.ld(INB[q_][hh], bg[h:h + 1, t0_:t0_ + TT].partition_broadcast(128), ("gdBG", ti_), key=lk)
                P.dma(gcol[:, q_, hh, 0, :], bg[32 + h, t0_:t0_ + TT].rearrange("(c t) -> t c", t=128),
                      reads=[self.dbuf("gdBG", ti_)], writes=[gcb_], key=lk, slow=True)
                P.dma(gcol[:, q_, hh, 1, :], bg[h, t0_:t0_ + TT].rearrange("(c t) -> t c", t=128),
                      reads=[self.dbuf("gdBG", ti_)], writes=[gcb_], key=lk, slow=True)
            P.group_end()
        if prefetch:
            issue(*seq[0])
        for kh in range(16):
            P.op("gpsimd", lambda e, o_=S[:, 0:256]: e.memset(o_, 0.0), writes=[S.b])
            for ti in range(self.NT):
                t0 = ti * TT
                q = ti % 2
                Qt, Kt = INQ[q], INK[q]
                gcb = gcol.bufs[q]
                si_ = kh * self.NT + ti
                if prefetch:
                    if si_ + 1 < len(seq):
                        issue(*seq[si_ + 1])
                else:
                    issue(kh, ti)
                for hh in range(2):
                    self.act(EGB[hh][:], ING[q][hh][:], AF.Exp, [ING[q][hh].b], [EGB[hh].b])
                    self.tt(self.ew(), QE[hh][:], Qt[:], EGB[hh][:], ALU.mult, [Qt.b, EGB[hh].b], [QE[hh].b])
                for srcv, dstv in ((Kt, KTOK), (INV[q][0], VTOK[0]), (INV[q][1], VTOK[1])):
                    ps = self.next_ps()
                    for ch in range(4):
                        cs_ = slice(ch * 128, (ch + 1) * 128)
                        self.tr(ps, ps[:, cs_], srcv[:, cs_], [srcv.b])
                    self.cp("scalar", dstv[:], ps[:], [ps.b], [dstv.b])
                for ch in range(4):
                    cs_ = slice(ch * 128, (ch + 1) * 128)
                    last = ch * 128 + 127
                    p = ch % 2
                    sm = SM[p]
                    self.cv_pump(1)
                    scb = gsc.bufs[p]
                    ps_kq = self.next_ps()
                    self.mm(ps_kq, ps_kq[:, 0:128], Kt[:, cs_], Kt[:, cs_], True, True, [Kt.b])
                    self.mm(ps_kq, ps_kq[:, 128:256], Kt[:, cs_], Qt[:, cs_], True, True, [Kt.b, Qt.b])
                    pw = PW[0]
                    for hh in range(2):
                        GB, BB = ING[q][hh], INB[q][hh]
                        g_c = gcol[:, q, hh, 0, ch:ch + 1]
                        b_c = gcol[:, q, hh, 1, ch:ch + 1]
                        eg = gsc[:, p, hh, 0:1]
                        beg = gsc[:, p, hh, 1:2]
                        egl = gsc[:, p, hh, 2:3]
                        self.act(eg, g_c, AF.Exp, [gcb], [scb])
                        self.tt("vector", beg, eg, b_c, ALU.mult, [scb, gcb], [scb])
                        self.act(egl, g_c, AF.Exp, [gcb, GB.b], [scb], scale=-1.0, bias=GB[:, last:last + 1])
                        o = hh * 384
                        self.ts("vector", sm[:, o:o + 128], KTOK[:, cs_], beg, None, ALU.mult, None, [KTOK.b, scb], sm.bufs)
                        self.ts("gpsimd", sm[:, o + 128:o + 256], VTOK[hh][:, cs_], b_c, None, ALU.mult, None, [VTOK[hh].b, gcb], sm.bufs)
                        self.ts("vector", sm[:, o + 256:o + 384], KTOK[:, cs_], egl, None, ALU.mult, None, [KTOK.b, scb], sm.bufs)
                        dd, at = DD[hh], AT[hh]
                        self.ts("vector", dd[:, 0:128], GB[:, cs_], g_c, None, ALU.subtract, None, [GB.b, gcb], [dd.b])
                        self.ts("gpsimd", dd[:, 384:512], dd[:, 0:128], 0.0, None, ALU.min, None, [dd.b], [dd.b])
                        self.act(dd[:, 128:256], dd[:, 384:512], AF.Exp, [dd.b], [dd.b])
                        self.ts("vector", dd[:, 384:512], dd[:, 0:128], -1.0, 0.0, ALU.mult, ALU.min, [dd.b], [dd.b])
                        self.act(dd[:, 256:384], dd[:, 384:512], AF.Exp, [dd.b], [dd.b])
                        self.tt("vector", at[:, 128:256], ps_kq[:, 0:128], dd[:, 128:256], ALU.mult, [ps_kq.b, dd.b], [at.b])
                        self.tt("gpsimd", at[:, 128:256], at[:, 128:256], BB[:, cs_], ALU.mult, [at.b, BB.b], [at.b])
                        self.tt("vector", pw[:, hh * 256 + 128:hh * 256 + 256], at[:, 128:256], negm[:, 0:128], ALU.mult,
                                [at.b, negm.b], [pw.b])
                        self.tt("vector", at[:, 256:384], ps_kq[:, 0:128], dd[:, 256:384], ALU.mult, [ps_kq.b, dd.b], [at.b])
                        self.stt("gpsimd", pw[:, hh * 256:hh * 256 + 128], at[:, 256:384], b_c, negm[:, 128:256], ALU.mult, ALU.mult,
                                 [at.b, gcb, negm.b], [pw.b])
                        self.tt("vector", at[:, 0:128], ps_kq[:, 128:256], dd[:, 128:256], ALU.mult, [ps_kq.b, dd.b], [at.b])
                        self.tt("gpsimd", at[:, 0:128], at[:, 0:128], mask2[:, 128:256], ALU.mult, [at.b, mask2.b], [at.b])
                        self.tt("gpsimd", Q[:, hh * 128:hh * 128 + 128], pw[:, hh * 256 + 128:hh * 256 + 256], ident[:], ALU.add,
                                [pw.b, ident.b], [Q.b])
                    self.neumann(PW, Q)
                    ps_uw = self.next_ps()
                    for hh in range(2):
                        o = hh * 384
                        self.mm(ps_uw, ps_uw[:, hh * 256:hh * 256 + 128], Q[:, hh * 128:hh * 128 + 128], sm[:, o + 128:o + 256], True, True,
                                [Q.b] + sm.bufs)
                        self.mm(ps_uw, ps_uw[:, hh * 256 + 128:hh * 256 + 256], sm[:, o:o + 128], Q[:, hh * 128:hh * 128 + 128], True, True,
                                [Q.b] + sm.bufs)
                    self.cp("vector", UW[:], ps_uw[:], [ps_uw.b], [UW.b])
                    ps_vn = self.next_ps()
                    for hh in range(2):
                        self.mm(ps_vn, ps_vn[:, hh * 128:hh * 128 + 128], UW[:, hh * 256 + 128:hh * 256 + 256], S[:, hh * 128:hh * 128 + 128],
                                True, True, [UW.b, S.b])
                    for hh in range(2):
                        self.tt("vector", VN[:, hh * 128:hh * 128 + 128], UW[:, hh * 256:hh * 256 + 128], ps_vn[:, hh * 128:hh * 128 + 128],
                                ALU.subtract, [UW.b, ps_vn.b], [VN.b])
                    ps_o = self.next_ps()
                    for hh in range(2):
                        self.mm(ps_o, ps_o[:, hh * 128:hh * 128 + 128], QE[hh][:, cs_], S[:, hh * 128:hh * 128 + 128], True, False,
                                [QE[hh].b, S.b])
                        self.mm(ps_o, ps_o[:, hh * 128:hh * 128 + 128], AT[hh][:, 0:128], VN[:, hh * 128:hh * 128 + 128], False, True,
                                [AT[hh].b, VN.b])
                    for hh in range(2):
                        self.cp("scalar", OTOK[hh][:, cs_], ps_o[:, hh * 128:hh * 128 + 128], [ps_o.b], [OTOK[hh].b])
                    ps_s = self.next_ps()
                    for hh in range(2):
                        o = hh * 384
                        self.mm(ps_s, ps_s[:, hh * 128:hh * 128 + 128], sm[:, o + 256:o + 384], VN[:, hh * 128:hh * 128 + 128], True, True,
                                sm.bufs + [VN.b])
                    for hh in range(2):
                        self.stt("vector", S[:, hh * 128:hh * 128 + 128], S[:, hh * 128:hh * 128 + 128], EGB[hh][:, last:last + 1],
                                 ps_s[:, hh * 128:hh * 128 + 128], ALU.mult, ALU.add, [S.b, EGB[hh].b, ps_s.b], [S.b])
                for hh in range(2):
                    h = 2 * kh + hh
                    O = OTOK[hh]
                    SQ, ms4 = PT[0], PT[1]
                    O3 = O[:].rearrange("p (g v) -> p g v", v=128)
                    S3 = SQ[:].rearrange("p (g v) -> p g v", v=128)
                    self.tt("gpsimd", SQ[:], O[:], O[:], ALU.mult, [O.b], [SQ.b])
                    P.op("vector", lambda e, o_=ms4[:, 0:4], i_=S3: e.reduce_sum(out=o_, in_=i_, axis=mybir.AxisListType.X),
                         reads=[SQ.b], writes=[ms4.b])
                    self.rsqrt(ms4[:, 0:4], ms4[:, 0:4], NORM_EPS, [ms4.b], [ms4.b], mul=1.0 / 128)
                    self.tt("vector", S3, O3, ms4[:, 0:4].unsqueeze(2).to_broadcast([128, 4, 128]), ALU.mult, [O.b, ms4.b], [SQ.b])
                    ps_t = self.next_ps()
                    for ch in range(4):
                        cs_ = slice(ch * 128, (ch + 1) * 128)
                        self.tr(ps_t, ps_t[:, cs_], SQ[:, cs_], [SQ.b])
                    OF = PT[2]
                    self.act(OF[:], ps_t[:], AF.Copy, [ps_t.b, nw.b], [OF.b], scale=nw[:, 0])
                    og = OGo[hh]
                    self.tt("vector", og[:, 0, :], OF[:], INZ[q][hh][:], ALU.mult, [OF.b, INZ[q][hh].b], [og.b])
                    self.P.dma(self.gd_og[h, :, t0:t0 + TT], og[:, 0, :], reads=[og.b], writes=[self.dbuf("gdOG", h, ti)], key=("st", og.b))

    def gdn_phase3(self, L, xin, xin_name, xout, xout_name):
        j = L // 2
        cv = self.cv
        X = self.f32(0, 16)
        OG = self.bf(16, 32)
        w_o = self.wspecs[f"gout{j}"]
        for ti in range(self.NT):
            t0 = ti * TT
            self.load_x(X, xin, xin_name, ti)
            self.P.group_begin()
            for c in range(32):
                self.P.dma(OG[:, c, :], self.gd_og[c, :, t0:t0 + TT], reads=[self.dbuf("gdOG", c, ti)], writes=[OG.bufs[c]],
                           key="ldog")
            self.P.group_end()
            for n in range(DC):
                ps = self.linear(w_o, n, lambda kc: (OG[:, kc, :], [OG.bufs[kc]]))
                self.stt("vector", X[:, n, :], X[:, n, :], ALPHA, ps[:], ALU.mult, ALU.add, [ps.b, X.bufs[n]], [X.bufs[n]])
            self.layernorm(X, cv[f"mlg{L}"], cv[f"mlb{L}"])
            self.store_x(X, xout, xout_name, ti)


class CV:
    def __init__(self, t, off, n):
        self.t, self.off, self.n = t, off, n
        self.b = t.b

    def __getitem__(self, k):
        assert isinstance(k, tuple) and len(k) == 2
        ps, cs = k
        if isinstance(cs, slice):
            a = 0 if cs.start is None else cs.start
            b = self.n if cs.stop is None else cs.stop
            return self.t[ps, self.off + a:self.off + b]
        return self.t[ps, self.off + cs:self.off + cs + 1]


def fm(v):
    v = np.asarray(v, np.float32).reshape(-1)
    return np.ascontiguousarray(v.reshape(-1, 128).T)


def const_list(plan, inp):
    out = []

    def add(name, ncol, f):
        out.append((name, ncol, (f() if inp is not None else None)))
    seen = set()
    for kind, L in plan:
        j = L // 2
        if kind == "ffn":
            add(f"fcw{L}", 3 * 2 * FC, lambda: np.concatenate([fm(inp["ffn_conv_w"][L, k]) for k in range(3)], axis=1))
            add(f"fcb{L}", 2 * FC, lambda: fm(inp["ffn_conv_b"][L]))
            add(f"flg{L}", DC, lambda: fm(inp["ln_ffn_g"][L]))
            add(f"flb{L}", DC, lambda: fm(inp["ln_ffn_b"][L]))
        else:
            add(f"mlg{L}", DC, lambda: fm(inp["ln_mix_g"][L]))
            add(f"mlb{L}", DC, lambda: fm(inp["ln_mix_b"][L]))
        if kind == "rwkv" and ("rw", j) not in seen:
            seen.add(("rw", j))
            add(f"rmu{j}", 6 * DC, lambda: np.concatenate([fm(inp["rw_mu"][j, k]) for k in range(6)], axis=1))
            add(f"rw0{j}", DC, lambda: fm(inp["rw_w0"][j]))
            add(f"ra0{j}", DC, lambda: fm(inp["rw_a0"][j]))
            if j > 0:
                add(f"rv0{j}", DC, lambda: fm(inp["rw_v0"][j - 1]))
            add(f"rkk{j}", DC, lambda: fm(inp["rw_k_k"][j]))
            add(f"rka{j}", DC, lambda: fm(inp["rw_k_a"][j]))
            add(f"romka{j}", DC, lambda: np.zeros((128, DC), np.float32))
            add(f"rrk{j}", DC, lambda: fm(inp["rw_r_k"][j]))
            add(f"rlg{j}", DC, lambda: fm(inp["rw_lnx_g"][j]))
            add(f"rlb{j}", DC, lambda: fm(inp["rw_lnx_b"][j]))
        if kind == "gdn" and ("gd", j) not in seen:
            seen.add(("gd", j))
            add(f"gcw{j}", 4 * 64, lambda: np.concatenate([fm(inp["gdn_conv_w"][j, k]) for k in range(4)], axis=1))
            add(f"gnw{j}", 1, lambda: np.asarray(inp["gdn_norm_w"][j], np.float32).reshape(128, 1))

            def padv(v):
                a = np.zeros((128, 1), np.float32)
                a[32:64, 0] = np.asarray(v, np.float32)
                return a
            add(f"gal{j}", 1, lambda: padv(inp["gdn_a_log"][j]))
            add(f"gdt{j}", 1, lambda: padv(inp["gdn_dt_bias"][j]))
            add(f"gnea{j}", 1, lambda: np.zeros((128, 1), np.float32))
    return out


def prep_w(W):
    K, N = W.shape
    KC, NCH = (K + 127) // 128, (N + 127) // 128
    Wp = np.zeros((KC * 128, NCH * 128), np.float32)
    Wp[:K, :N] = W
    return np.ascontiguousarray(Wp.reshape(KC, 128, NCH, 128).transpose(2, 1, 0, 3).reshape(NCH, 128, KC * 128))


def weight_list(plan, inp):
    out = []

    def add(name, K, N, f):
        out.append((name, K, N, (prep_w(f()) if inp is not None else None)))
    for kind, L in plan:
        j = L // 2
        if kind == "ffn":
            add(f"up{L}", D, 2 * DFF, lambda: inp["ffn_w_up"][L])
            add(f"dn{L}", DFF, D, lambda: inp["ffn_w_down"][L])
        elif kind == "rwkv":
            add(f"rwr{j}", D, D, lambda: inp["rw_w_r"][j])
            add(f"rwk{j}", D, D, lambda: inp["rw_w_k"][j])
            add(f"rwv{j}", D, D, lambda: inp["rw_w_v"][j])
            add(f"rwo{j}", D, D, lambda: inp["rw_w_o"][j])
            add(f"rw1{j}", D, 96, lambda: inp["rw_w1"][j])
            add(f"rw2{j}", 96, D, lambda: inp["rw_w2"][j])
            add(f"ra1{j}", D, 96, lambda: inp["rw_a1"][j])
            add(f"ra2{j}", 96, D, lambda: inp["rw_a2"][j])
            add(f"rg1{j}", D, 256, lambda: inp["rw_g1"][j])
            add(f"rg2{j}", 256, D, lambda: inp["rw_g2"][j])
            if j > 0:
                add(f"rv1{j}", D, 64, lambda: inp["rw_v1"][j - 1])
                add(f"rv2{j}", 64, D, lambda: inp["rw_v2"][j - 1])
        elif kind == "gdn":
            add(f"gin{j}", D, 12352, lambda: inp["gdn_w_in"][j])
            add(f"gout{j}", 4096, D, lambda: inp["gdn_w_out"][j])
    return out


def build(T, plan):
    B = Builder(T, plan)
    nc, P = B.nc, B.P
    x_in = B.din("xT", [DC, 128, T])
    out = nc.dram_tensor("outT", [DC, 128, T], F32, kind="ExternalOutput").ap()
    xs = [B.dscratch("xs0", [DC, 128, T]), B.dscratch("xs1", [DC, 128, T])]
    cl = const_list(plan, None)
    ntot = sum(n for _, n, _ in cl)
    cst_in = B.din("consts", [128, ntot])
    cst = P.tile("consts_sb", [128, ntot], F32)
    P.dma(cst[:], cst_in, writes=[cst.b], key="consts")
    B.cv = {}
    off = 0
    for name, n, _ in cl:
        B.cv[name] = CV(cst, off, n)
        off += n
    for name, K, N, _ in weight_list(plan, None):
        KC, NCH = (K + 127) // 128, (N + 127) // 128
        B.wspec(name, B.din("w_" + name, [NCH, 128, KC * 128]), K, N)

    B.setup_common()
    B.cv_init()
    stage_w = {}
    for kind, L in plan:
        stage_w[(kind, L)] = [nm for nm, _, _, _ in weight_list([(kind, L)], None)]

    cur, cur_name = x_in, "xin"
    nxt_i = 0
    for si, (kind, L) in enumerate(plan):
        last = si == len(plan) - 1
        dst, dst_name = (out, "out") if last else (xs[nxt_i], f"xs{nxt_i}")
        B.cv_drain(stage_w[(kind, L)])
        if kind == "ffn":
            B.ffn_stage(L, cur, cur_name, dst, dst_name)
        elif kind == "rwkv":
            B.rwkv_stage(L, cur, cur_name, dst, dst_name)
        elif kind == "gdn":
            B.gdn_stage(L, cur, cur_name, dst, dst_name)
        cur, cur_name = dst, dst_name
        nxt_i ^= 1
    P.emit()
    return nc, B


def host_inputs(plan, inp, xb):
    m = {}
    T = xb.shape[0]
    m["xT"] = np.ascontiguousarray(xb.T.reshape(DC, 128, T))
    cl = const_list(plan, inp)
    m["consts"] = np.ascontiguousarray(np.concatenate([a for _, _, a in cl], axis=1).astype(np.float32))
    for name, K, N, a in weight_list(plan, inp):
        m["w_" + name] = a
    return m


FULL_PLAN = [("rwkv", 0), ("ffn", 0), ("gdn", 1), ("ffn", 1), ("rwkv", 2), ("ffn", 2), ("gdn", 3), ("ffn", 3)]


def kernel(**inputs):
    inp = {k: np.asarray(v) for k, v in inputs.items()}
    x = inp["x"]
    Bn, T, _ = x.shape
    nc, B = build(T, FULL_PLAN)
    shared = host_inputs(FULL_PLAN, inp, x[0])
    in_maps = []
    for i in range(Bn):
        m = dict(shared)
        m["xT"] = np.ascontiguousarray(x[i].T.reshape(DC, 128, T))
        in_maps.append(m)
    res = run_bass_kernel_spmd(nc, in_maps, core_ids=list(range(Bn)))
    outs = [np.asarray(r["outT"]).reshape(D, T).T for r in res.results]
    return np.ascontiguousarray(np.stack(outs, axis=0).astype(np.float32))
```
